# Optimizing a Trainium2 kernel written in Bass

```python
import jax, jax.numpy as jnp
from jax import lax
import numpy as np

D_MODEL = 1024
BATCH = 8
SEQ = 2048
DEPTH = 2
DEC_BATCH = 128
DEC_SEQ = 8
PAST_LEN = 16384
PAGE_SIZE = 128

N_META = 16
RET_HEADS = 8
RET_DK = D_MODEL // RET_HEADS
RET_DV = D_MODEL // RET_HEADS
RET_QK = RET_HEADS * RET_DK
RET_V = RET_HEADS * RET_DV
RET_CHUNK = 128
ROPE_BASE = 10000.0
D_CONV = D_MODEL
CONV_W = 3
D_FF = ((8 * D_MODEL + 3 * 256 - 1) // (3 * 256)) * 256
EPS = 1e-6
SPLITS = (RET_QK, RET_QK, RET_V, RET_V, D_CONV, D_CONV, D_CONV, D_MODEL, D_MODEL)
N_IN = RET_QK * 2 + RET_V * 2 + D_CONV * 3 + D_MODEL * 2

kernel_name = "retention_shortconv_gated_hybrid_step"


def _rms(x, g):
    x32 = x.astype(jnp.float32)
    y = x32 * lax.rsqrt(jnp.mean(x32 * x32, axis=-1, keepdims=True) + EPS)
    return (y * g.astype(jnp.float32)).astype(x.dtype)


def _split(proj):
    outs, start = [], 0
    for w in SPLITS:
        outs.append(proj[..., start:start + w])
        start += w
    return outs


def _rotary(t, pos):
    half = t.shape[-1] // 2
    inv = jnp.power(ROPE_BASE, -jnp.arange(half, dtype=jnp.float32) / half)
    ang = pos[:, None] * inv[None, :]
    cos, sin = jnp.cos(ang), jnp.sin(ang)
    t1, t2 = t[..., :half], t[..., half:]
    return jnp.concatenate([t1 * cos - t2 * sin, t1 * sin + t2 * cos], axis=-1)


def _log_gamma():
    return jnp.log1p(-jnp.exp2(-5.0 - jnp.arange(RET_HEADS, dtype=jnp.float32)))


def _ret_chunk(q, k, v, s, log_g):
    L = q.shape[2]
    idx = jnp.arange(L, dtype=jnp.float32)
    diff = idx[:, None] - idx[None, :]
    dmask = jnp.where(diff >= 0, jnp.exp(log_g[:, None, None] * jnp.maximum(diff, 0.0)[None]), 0.0)
    scores = jnp.einsum("bhid,bhjd->bhij", q, k) * dmask[None]
    inner = jnp.einsum("bhij,bhje->bhie", scores, v)
    qdec = jnp.exp(log_g[:, None] * (idx + 1.0)[None])[None, :, :, None]
    cross = jnp.einsum("bhid,bhde->bhie", q, s) * qdec
    kdec = jnp.exp(log_g[:, None] * (L - 1.0 - idx)[None])[None, :, :, None]
    s_new = jnp.exp(log_g * L)[None, :, None, None] * s + jnp.einsum("bhjd,bhje->bhde", k * kdec, v)
    return inner + cross, s_new


def _retention(q, k, v, s0, lead):
    log_g = _log_gamma()
    b, h, L, _ = q.shape
    o_lead, s = _ret_chunk(q[:, :, :lead], k[:, :, :lead], v[:, :, :lead], s0, log_g)
    rest = L - lead
    if rest == 0:
        return o_lead, s
    nc = rest // RET_CHUNK

    def blocks(t):
        return jnp.moveaxis(t[:, :, lead:].reshape(b, h, nc, RET_CHUNK, t.shape[-1]), 2, 0)

    def step(st, qkv):
        o, st = _ret_chunk(qkv[0], qkv[1], qkv[2], st, log_g)
        return st, o

    s, o_rest = lax.scan(step, s, (blocks(q), blocks(k), blocks(v)))
    o_rest = jnp.moveaxis(o_rest, 0, 2).reshape(b, h, rest, v.shape[-1])
    return jnp.concatenate([o_lead, o_rest], axis=2), s


def _mixer(hn, pos, s0, cprev, lead, w_in, conv_w, w_ret_o, w_conv_o, w_o):
    b, L, _ = hn.shape
    proj = hn @ w_in
    q, k, v, g, bg, cg, hc, ga, gb = _split(proj)

    def heads(t, d):
        return t.reshape(b, L, RET_HEADS, d).transpose(0, 2, 1, 3).astype(jnp.float32)

    qh = _rotary(heads(q, RET_DK), pos)
    kh = _rotary(heads(k, RET_DK), pos) * (RET_DK ** -0.5)
    vh = heads(v, RET_DV)
    o, s_new = _retention(qh, kh, vh, s0.astype(jnp.float32), lead)
    mu = jnp.mean(o, axis=-1, keepdims=True)
    var = jnp.mean(jnp.square(o - mu), axis=-1, keepdims=True)
    o = ((o - mu) * lax.rsqrt(var + EPS)).transpose(0, 2, 1, 3).reshape(b, L, RET_V).astype(hn.dtype)
    ret_out = (jax.nn.silu(g) * o) @ w_ret_o

    u = cg * hc
    full = jnp.concatenate([cprev.astype(u.dtype), u], axis=1)
    y = conv_w[0] * full[:, 0:L]
    for j in range(1, CONV_W):
        y = y + conv_w[j] * full[:, j:j + L]
    conv_out = (bg * y) @ w_conv_o

    merged = jax.nn.sigmoid(ga) * ret_out + jax.nn.sigmoid(gb) * conv_out
    return merged @ w_o, s_new, full[:, -(CONV_W - 1):]


def _ffn(hn, w_gate_up, w_down):
    gu = hn @ w_gate_up
    return (jax.nn.silu(gu[..., :D_FF]) * gu[..., D_FF:]) @ w_down


def _trunk(h, pos, s_in, c_in, lead, norm_mix_g, w_in, conv_w, w_ret_o, w_conv_o, w_o,
           norm_ffn_g, w_gate_up, w_down, final_norm_g):
    s_out, c_out = [], []
    for l in range(DEPTH):
        m, s, c = _mixer(_rms(h, norm_mix_g[l]), pos, s_in[l], c_in[l], lead,
                         w_in[l], conv_w[l], w_ret_o[l], w_conv_o[l], w_o[l])
        h = h + m
        h = h + _ffn(_rms(h, norm_ffn_g[l]), w_gate_up[l], w_down[l])
        s_out.append(s)
        c_out.append(c)
    return _rms(h, final_norm_g), jnp.stack(s_out), jnp.stack(c_out)


def setup_inputs(seed: int = 0) -> dict:
    key = jax.random.key(seed)
    ks = jax.random.split(key, 16)
    f32 = jnp.float32
    n = lambda k, shape, s: jax.random.normal(k, shape, f32) * s
    return {
        "x_prompt": n(ks[0], (BATCH, SEQ, D_MODEL), 1.0),
        "x_sample": n(ks[1], (DEC_BATCH, DEC_SEQ, D_MODEL), 1.0),
        "state_ret": n(ks[2], (DEPTH, DEC_BATCH, RET_HEADS, RET_DK, RET_DV), 0.5),
        "state_conv": n(ks[3], (DEPTH, DEC_BATCH, CONV_W - 1, D_CONV), 1.0),
        "meta_tokens": n(ks[4], (N_META, D_MODEL), 1.0),
        "norm_mix_g": 1.0 + n(ks[5], (DEPTH, D_MODEL), 0.05),
        "w_in": n(ks[6], (DEPTH, D_MODEL, N_IN), D_MODEL ** -0.5),
        "conv_w": n(ks[7], (DEPTH, CONV_W, D_CONV), CONV_W ** -0.5),
        "w_ret_o": n(ks[8], (DEPTH, RET_V, D_MODEL), RET_V ** -0.5),
        "w_conv_o": n(ks[9], (DEPTH, D_CONV, D_MODEL), D_CONV ** -0.5),
        "w_o": n(ks[10], (DEPTH, D_MODEL, D_MODEL), D_MODEL ** -0.5),
        "norm_ffn_g": 1.0 + n(ks[11], (DEPTH, D_MODEL), 0.05),
        "w_gate_up": n(ks[12], (DEPTH, D_MODEL, 2 * D_FF), D_MODEL ** -0.5),
        "w_down": n(ks[13], (DEPTH, D_FF, D_MODEL), D_FF ** -0.5),
        "final_norm_g": 1.0 + n(ks[14], (D_MODEL,), 0.05),
    }


def reference(x_prompt, x_sample, state_ret, state_conv, meta_tokens, norm_mix_g, w_in, conv_w,
              w_ret_o, w_conv_o, w_o, norm_ffn_g, w_gate_up, w_down, final_norm_g):
    weights = (norm_mix_g, w_in, conv_w, w_ret_o, w_conv_o, w_o, norm_ffn_g, w_gate_up, w_down, final_norm_g)

    meta = jnp.broadcast_to(meta_tokens[None].astype(x_prompt.dtype), (BATCH, N_META, D_MODEL))
    h_p = jnp.concatenate([meta, x_prompt], axis=1)
    pos_p = jnp.arange(N_META + SEQ, dtype=jnp.float32)
    s0_p = jnp.zeros((DEPTH, BATCH, RET_HEADS, RET_DK, RET_DV), state_ret.dtype)
    c0_p = jnp.zeros((DEPTH, BATCH, CONV_W - 1, D_CONV), state_conv.dtype)
    y_p, s_p, c_p = _trunk(h_p, pos_p, s0_p, c0_p, N_META, *weights)
    y_prompt = y_p[:, N_META:]

    pos_s = PAST_LEN + jnp.arange(DEC_SEQ, dtype=jnp.float32)
    y_sample, s_s, c_s = _trunk(x_sample, pos_s, state_ret, state_conv, DEC_SEQ, *weights)

    return (y_prompt, y_sample, s_p.astype(state_ret.dtype), c_p.astype(state_conv.dtype),
            s_s.astype(state_ret.dtype), c_s.astype(state_conv.dtype))
```

```python
import numpy as np
from contextlib import ExitStack
import concourse.bass as bass
import concourse.mybir as mybir
from concourse.bass_utils import run_bass_kernel_spmd

F32 = mybir.dt.float32
BF16 = mybir.dt.bfloat16
AF = mybir.ActivationFunctionType
ALU = mybir.AluOpType
AX = mybir.AxisListType

D = 1024
NK = 8
HEADS = 8
DFF = 2816
NF = 22
N_META = 16
SEQ = 2048
PAST = 16384
NCORES = 8
NSS = 16
DS = 8
EPS = 1e-6
DEPTH = 2

SGS = [
    dict(T=656, ngs=[(0, 144), (144, 512)],
         chunks=[("sample", 0, 128), ("lead", 128, 16)] + [("full", 144 + 128 * i, 128) for i in range(4)],
         pstart=128, plen=528, p0=None),
    dict(T=768, ngs=[(0, 384), (384, 384)], chunks=[("full", 128 * i, 128) for i in range(6)],
         pstart=0, plen=768, p0=512),
    dict(T=768, ngs=[(0, 384), (384, 384)], chunks=[("full", 128 * i, 128) for i in range(6)],
         pstart=0, plen=768, p0=1280),
]
TMAX = 768
NCHMAX = 6

C_Q, C_K, C_V, C_G, C_BG, C_CG, C_HC, C_GA, C_GB = [i * 1024 for i in range(9)]


def weight_groups():
    g = []
    for kind in ("wk", "wq", "wv", "wg"):
        g.append((kind, 0, 4096))
    for kind in ("wk", "wq", "wv", "wg"):
        g.append((kind, 1, 4096))
    for c in range(8):
        g.append(("conv", c, 3072))
    for mp in range(4):
        g.append(("garo", mp, 4096))
    for mp in range(4):
        g.append(("gbco", mp, 4096))
    for i in range(2):
        g.append(("wo", i, 4096))
    for fp in range(11):
        g.append(("ffn1", fp, 4096))
    for m in range(8):
        g.append(("wd", m, 2816))
    return g


WG = weight_groups()
WUNIQ = []
for _g in WG:
    if _g not in WUNIQ:
        WUNIQ.append(_g)
_uoff = np.cumsum([0] + [x[2] for x in WUNIQ]).tolist()
WTOT = _uoff[-1]
WOFF = [_uoff[WUNIQ.index(_g)] for _g in WG]
NSLOT = 3
import os
DBG = os.environ.get('KDBG') == '1'
POLICY = os.environ.get('KPOL', 'fixed')
MINS = int(os.environ.get('KMINS', '1'))
MARGIN = float(os.environ.get('KMARGIN', '0.5'))
SLO = float(os.environ.get('KSLO', '0'))
HOP = float(os.environ.get('KHOP', '0.8'))
KDVE = float(os.environ.get('KDVE', '1'))
KACT = float(os.environ.get('KACT', '1'))
KPOOL = float(os.environ.get('KPOOL', '1'))
SIDE_RATIO = int(os.environ.get('KSR', '2'))
MERGE01 = os.environ.get('KM01', '0') == '1'
MERGE3 = os.environ.get('KM3', '0') == '1'
POOL_ENG = os.environ.get('KPOOLENG', 'pool')
GSIDE = os.environ.get('KGSIDE', '1') == '1'
NSET_S = int(os.environ.get('KNSETS', '4'))
SG0_EARLY = os.environ.get('KSG0E', '0') == '1'
SPLIT_SAMPLE = os.environ.get('KSPLIT', '0') == '1'
RMS_LAG = int(os.environ.get('KRMSLAG', '2'))
SHI = float(os.environ.get('KSHI', '0'))
SLOTSZ = 4096


def _unit(W, c0):
    return W[:, c0:c0 + 128].reshape(8, 128, 128).transpose(1, 0, 2).reshape(128, 1024)


def pack_weights(w_in, w_ret_o, w_conv_o, w_o, w_gate_up, w_down):
    out = np.empty((DEPTH, 128, WTOT), np.float32)
    for l in range(DEPTH):
        col = 0
        for kind, idx, sz in WUNIQ:
            if kind == "wv":
                blk = w_in[l][:, C_V + idx * 512:C_V + (idx + 1) * 512].reshape(8, 128, 512).transpose(1, 0, 2).reshape(128, 4096)
            elif kind in ("wq", "wk", "wg"):
                base = {"wq": C_Q, "wk": C_K, "wg": C_G}[kind]
                blk = np.concatenate([_unit(w_in[l], base + (idx * 4 + j) * 128) for j in range(4)], axis=1)
            elif kind == "garo":
                blk = np.concatenate([_unit(w_in[l], C_GA + (2 * idx) * 128), _unit(w_ret_o[l], (2 * idx) * 128),
                                      _unit(w_in[l], C_GA + (2 * idx + 1) * 128), _unit(w_ret_o[l], (2 * idx + 1) * 128)], axis=1)
            elif kind == "conv":
                blk = np.concatenate([_unit(w_in[l], C_CG + idx * 128), _unit(w_in[l], C_HC + idx * 128),
                                      _unit(w_in[l], C_BG + idx * 128)], axis=1)
            elif kind == "gbco":
                blk = np.concatenate([_unit(w_in[l], C_GB + (2 * idx) * 128), _unit(w_conv_o[l], (2 * idx) * 128),
                                      _unit(w_in[l], C_GB + (2 * idx + 1) * 128), _unit(w_conv_o[l], (2 * idx + 1) * 128)], axis=1)
            elif kind == "wo":
                blk = np.concatenate([_unit(w_o[l], (4 * idx + j) * 128) for j in range(4)], axis=1)
            elif kind == "ffn1":
                blk = np.concatenate([_unit(w_gate_up[l], (2 * idx) * 128), _unit(w_gate_up[l], DFF + (2 * idx) * 128),
                                      _unit(w_gate_up[l], (2 * idx + 1) * 128), _unit(w_gate_up[l], DFF + (2 * idx + 1) * 128)], axis=1)
            elif kind == "wd":
                blk = w_down[l][:, idx * 128:(idx + 1) * 128].reshape(NF, 128, 128).transpose(1, 0, 2).reshape(128, NF * 128)
            out[l][:, col:col + sz] = blk
            col += sz
    return out


class Src:
    def __init__(self, name, mult):
        self.name, self.mult, self.count, self.sem = name, mult, 0, None


class Buf:
    __slots__ = ("name", "w", "r")

    def __init__(self, name):
        self.name, self.w, self.r = name, None, {}


class Prog:
    ENG = ("pe", "act", "dve", "pool", "sp")

    def __init__(self):
        self.src = {e: Src(e, 1) for e in self.ENG}
        self.ops = {e: [] for e in self.ENG}
        self.seen = {e: {} for e in self.ENG}
        self.chans = []
        self.t = {e: 0.0 for e in self.ENG}
        self.fin = {}

    def chan(self, name):
        c = Src(name, 16)
        self.chans.append(c)
        return c

    def _deps(self, eng, reads, writes):
        deps = {}

        def add(s, c):
            if deps.get(s, 0) < c:
                deps[s] = c
        for b in reads:
            if b.w is not None:
                add(*b.w)
        for b in writes:
            if b.w is not None:
                add(*b.w)
            for s, c in b.r.items():
                add(s, c)
        me = self.src[eng]
        seen = self.seen[eng]
        out = []
        self._ready = max([self.fin.get((s, c), 0.0) for s, c in deps.items() if s is not me] + [0.0])
        self._alldeps = deps
        for s, c in deps.items():
            if s is me:
                continue
            if seen.get(s, 0) >= c:
                continue
            seen[s] = c
            out.append((s, c))
        if eng in ("act", "dve", "pool"):
            c = 0
            for b in reads:
                if b.w is not None and b.w[0] is me:
                    c = max(c, b.w[1])
            if c > seen.get(me, 0):
                seen[me] = c
                out.append((me, c))
        return out

    def op(self, eng, fn, reads=(), writes=(), signal=True, cost=0.3):
        waits = self._deps(eng, reads, writes)
        me = self.src[eng]
        cnt = me.count + 1
        if signal:
            me.count = cnt
        start = max(self.t[eng], self._ready + HOP)
        if DBG and eng == "pe" and start - self.t[eng] > 1.0 and SLO < start < SHI:
            blk = max(((self.fin.get((s_, c_), 0.0), s_.name) for s_, c_ in self._alldeps.items() if s_ is not me), default=None)
            print(f"[sim] PE stall {self.t[eng]:.1f} -> {start:.1f} ({start - self.t[eng]:.1f}us) waiting on {blk} writes={[b.name for b in writes]} reads={[b.name for b in reads][:3]}")
        self.t[eng] = start + cost
        self.fin[(me, cnt)] = self.t[eng]
        self.busy = getattr(self, "busy", {})
        self.busy[eng] = self.busy.get(eng, 0.0) + cost
        for b in reads:
            b.r[me] = cnt
        for b in writes:
            b.w = (me, cnt)
            b.r = {}
        self.ops[eng].append((waits, fn, me if signal else None, 1))

    def dma(self, queue, fn, chan, reads=(), writes=(), cost=4.0):
        waits = self._deps(queue, reads, writes)
        chan.count += 1
        start = max(self.t[queue], self._ready + HOP)
        self.t[queue] = start + 0.1
        self.fin[(chan, chan.count)] = start + cost
        for b in reads:
            b.r[chan] = chan.count
        for b in writes:
            b.w = (chan, chan.count)
            b.r = {}
        self.ops[queue].append((waits, fn, chan, 16))

    def barrier(self, engines=("pe", "act", "dve", "sp", "pool")):
        for e in engines:
            waits = []
            for s in list(self.src.values()) + self.chans:
                if s is self.src[e] or s.count == 0:
                    continue
                if self.seen[e].get(s, 0) >= s.count:
                    continue
                self.seen[e][s] = s.count
                waits.append((s, s.count))
            if waits:
                self.ops[e].append((waits, None, None, 0))

    def emit(self, block):
        decos = {"pe": block.tensor, "act": block.scalar, "dve": block.vector, "pool": block.gpsimd, "sp": block.sync}
        for eng in self.ENG:
            ops = self.ops[eng]

            def body(e, ops=ops):
                for waits, fn, sig, inc in ops:
                    for s, c in waits:
                        e.wait_ge(s.sem, c * s.mult)
                    if fn is None:
                        continue
                    ins = fn(e)
                    if sig is not None:
                        ins.then_inc(sig.sem, inc)
            decos[eng](body)


def build_program():
    nc = bass.Bass("TRN2", target_bir_lowering=False)
    P = Prog()
    es = ExitStack()

    def din(name, shape):
        return nc.dram_tensor(name, list(shape), F32, kind="ExternalInput").ap()

    def dout(name, shape):
        return nc.dram_tensor(name, list(shape), F32, kind="ExternalOutput").ap()

    x_in = [din(f"x{i}", (128, NK, SGS[i]["T"])) for i in range(3)]
    cs_in = [din(f"cs{i}", (128, 2, SGS[i]["T"])) for i in range(3)]
    wpk = din("wpk", (DEPTH, 128, WTOT))
    dmask_in = din("dmask", (128, HEADS, 128))
    smask_in = din("smask", (128, HEADS, 128))
    qdec_in = din("qdec", (128, HEADS, 128))
    qdecs_in = din("qdecs", (128, HEADS, 128))
    kdec_in = din("kdec", (128, 3, HEADS))
    rm3_in = din("rm3", (128, NSS, 128))
    ident_in = din("ident", (128, 128))
    gains_in = din("gains", (128, 5, NK))
    convw_in = din("convw", (128, DEPTH, NK, 3))
    sret_in = din("sret", (DEPTH, 128, HEADS, NSS, 128))
    sconv_in = din("sconv", (128, DEPTH, NK, NSS, 2))
    y_out = [dout(f"y{i}", (128, NK, SGS[i]["T"])) for i in range(3)]
    osp = dout("osp", (DEPTH, 128, 1024))
    ocp = dout("ocp", (128, DEPTH, NK, 2))
    oss = dout("oss", (DEPTH, 128, HEADS, NSS, 128))
    ocs = dout("ocs", (128, DEPTH, NK, NSS, 2))

    uniq = [0]

    def sb(name, shape, dt=F32, stack=None):
        uniq[0] += 1
        return (stack or es).enter_context(nc.sbuf_tensor(f"{name}_{uniq[0]}", list(shape), dt))

    def ps(name, shape, dt=F32):
        return es.enter_context(nc.psum_tensor(name, list(shape), dt))

    H = sb("H", (128, NK, TMAX))
    A = sb("A", (128, NK, TMAX), BF16)
    O = sb("O", (128, NK, TMAX), BF16)
    SST = sb("SST", (128, DEPTH, 1024))
    SBF = sb("SBF", (128, DEPTH, 1024), BF16)
    CARRY = sb("CARRY", (128, DEPTH, NK, 2))
    CSS = sb("CSS", (128, DEPTH, NK, NSS, 2))
    CPV = sb("CPV", (128, DEPTH, NK, NSS, 2))
    WSL = [sb(f"WSL{i}", (128, SLOTSZ), BF16) for i in range(NSLOT)]
    CS = sb("CS", (128, 2, TMAX))
    DMASK = sb("DMASK", (128, HEADS, 128))
    QDEC = sb("QDEC", (128, HEADS, 128))
    KDEC = sb("KDEC", (128, 3, HEADS))
    IDENT = sb("IDENT", (128, 128), BF16)
    ONES = sb("ONES", (128, 128), BF16)
    GAINS = sb("GAINS", (128, 5, NK))
    CONVW = sb("CONVW", (128, DEPTH, NK, 3))
    EPSC = sb("EPSC", (128, 1))

    PS = [ps(f"PS{i}", (128, 512)) for i in range(8)]
    bPS = [Buf(f"PS{i}") for i in range(8)]

    bH = [[Buf(f"H{k}_{g}") for g in range(2)] for k in range(NK)]
    bA = [[Buf(f"A{k}_{g}") for g in range(2)] for k in range(NK)]

    class AK:
        def __init__(self, g):
            self.g = g
    bO = [[Buf(f"O{h}_{g}") for g in range(2)] for h in range(HEADS)]
    bSST = [[Buf(f"SST{l}_{hh}") for hh in range(2)] for l in range(DEPTH)]
    bSBF = [[Buf(f"SBF{l}_{hh}") for hh in range(2)] for l in range(DEPTH)]
    bCARRY = [[Buf(f"CARRY{l}_{c}") for c in range(NK)] for l in range(DEPTH)]
    bCSS = [[Buf(f"CSS{l}_{c}") for c in range(NK)] for l in range(DEPTH)]
    bWSL = [Buf(f"WSL{i}") for i in range(NSLOT)]
    bCS = Buf("CS")
    bCONST = Buf("CONST")

    ch_w = [P.chan(f"w{i}") for i in range(NSLOT)]
    ch_const = P.chan("const")
    ch_constsw = P.chan("constsw")
    ch_sc = P.chan("sc")
    ch_scsw = P.chan("scsw")
    ch_x = P.chan("x")
    ch_cs = P.chan("cs")
    ch_y = P.chan("y")
    ch_small = P.chan("small")
    ch_sin = [P.chan(f"sin{i}") for i in range(4)]
    ch_sinbf = [P.chan(f"sinbf{i}") for i in range(2)]
    ch_sout = [P.chan(f"sout{i}") for i in range(4)]

    def fsz(ap):
        n_ = 1
        for d_ in ap.shape[1:]:
            n_ *= d_
        return n_

    def ecost(eng, ap):
        f_ = fsz(ap)
        if eng == "dve":
            return (f_ / 960.0 + 0.07) * KDVE
        if eng == "act":
            return (f_ / 1200.0 + 0.22) * KACT
        return (f_ * 0.0022 + 0.15) * KPOOL

    def mm(out, lhsT, rhs, start, stop, reads, writes, signal):
        P.op("pe", lambda e: e.matmul(out, lhsT=lhsT, rhs=rhs, start=start, stop=stop), reads, writes, signal,
             cost=max(fsz(out), 64) / 2400.0 + 0.008)

    def tr(out, in_, ident, reads, writes, signal=True):
        P.op("pe", lambda e: e.transpose(out, in_, ident), list(reads) + [bIDENT], writes, signal, cost=0.07)

    def tt(eng, out, in0, in1, op, reads, writes):
        P.op(eng, lambda e: e.tensor_tensor(out=out, in0=in0, in1=in1, op=op), reads, writes, cost=ecost(eng, out))

    def ts(eng, out, in0, s1, s2, op0, op1, reads, writes):
        if s2 is None:
            P.op(eng, lambda e: e.tensor_scalar(out=out, in0=in0, scalar1=s1, scalar2=None, op0=op0), reads, writes, cost=ecost(eng, out))
        else:
            P.op(eng, lambda e: e.tensor_scalar(out=out, in0=in0, scalar1=s1, scalar2=s2, op0=op0, op1=op1), reads, writes, cost=ecost(eng, out))

    def stt(eng, out, in0, scalar, in1, op0, op1, reads, writes):
        P.op(eng, lambda e: e.scalar_tensor_tensor(out=out, in0=in0, scalar=scalar, in1=in1, op0=op0, op1=op1), reads, writes, cost=ecost(eng, out))

    def act(out, in_, func, reads, writes, bias=None, scale=None):
        kw = {}
        if bias is not None:
            kw["bias"] = bias
        if scale is not None:
            kw["scale"] = scale
        P.op("act", lambda e: e.activation(out=out, in_=in_, func=func, **kw), reads, writes, cost=ecost("act", out))

    def cp(eng, out, in_, reads, writes):
        if eng == "act":
            P.op("act", lambda e: e.copy(out=out, in_=in_), reads, writes, cost=ecost("act", out))
        else:
            P.op(eng, lambda e: e.tensor_copy(out=out, in_=in_), reads, writes, cost=ecost(eng, out))

    def dma(queue, out, in_, chan, reads, writes):
        nbytes = 128 * fsz(in_) * 4
        P.dma(queue, lambda e: e.dma_start(out=out, in_=in_), chan, reads, writes, cost=2.0 + nbytes / 340e3)

    for dst, src in ((DMASK, dmask_in), (QDEC, qdec_in), (KDEC, kdec_in), (GAINS, gains_in), (CONVW, convw_in), (CPV, sconv_in)):
        dma("sp", dst[:], src[:], ch_const, [], [])
    bCONST.w = (ch_const, ch_const.count)
    P.fin[(ch_const, ch_const.count)] = max(P.fin.get((ch_const, c_), 0.0) for c_ in range(1, ch_const.count + 1))
    bIDENT = Buf("IDENT")
    dma("pool", IDENT[:], ident_in[:], ch_constsw, [], [bIDENT])
    P.op("dve", lambda e: e.memset(ONES[:], 1.0), [], [bCONST])
    P.op("dve", lambda e: e.memset(EPSC[:], EPS), [], [bCONST])
    P.op("dve", lambda e: e.memset(CSS[:], 0.0), [], [bCONST])

    def wg_order(sgi_):
        idx = list(range(len(WG)))
        if SG0_EARLY and sgi_ == 0:
            want = [("wk", 0), ("wq", 0), ("wk", 1), ("wq", 1), ("wv", 0), ("wv", 1), ("wg", 0), ("wg", 1)]
            head = [next(i for i in idx if WG[i][0] == k_ and WG[i][1] == h_) for k_, h_ in want]
            idx = head + [i for i in idx if i not in head]
        return idx
    wseq = [(l, gi) for sg_ in range(3) for l in range(DEPTH) for gi in wg_order(sg_)]
    wstate = {"issued": 0, "cur": -1}

    def w_issue_upto(n):
        while wstate["issued"] < min(n, len(wseq)):
            i = wstate["issued"]
            l, gi = wseq[i]
            sz = WG[gi][2]
            s = i % NSLOT
            dma("pool", WSL[s][:, 0:sz], wpk[l][:, WOFF[gi]:WOFF[gi] + sz], ch_w[s], [], [bWSL[s]])
            wstate["issued"] += 1

    def w_next(expect_kind):
        wstate["cur"] += 1
        i = wstate["cur"]
        assert WG[wseq[i][1]][0] == expect_kind, (WG[wseq[i][1]], expect_kind)
        w_issue_upto(i + NSLOT)
        s = i % NSLOT
        return WSL[s], bWSL[s]

    w_issue_upto(NSLOT)

    class AccPool:
        def __init__(self, idxs):
            self.idxs, self.i = idxs, 0

        def next(self):
            k = self.idxs[self.i % len(self.idxs)]
            self.i += 1
            return PS[k], bPS[k]

    def ng_of(sg, t0):
        for gi, (a, n) in enumerate(sg["ngs"]):
            if a <= t0 < a + n:
                return gi
        raise AssertionError

    def dense_acc(acc, bacc, wsl, bw, woff, nkc, X, xk_stride_fn, a, n, bx):
        for kc in range(nkc):
            bxk = [bA[kc][bx.g]] if isinstance(bx, AK) else bx
            mm(acc[:, 0:n], wsl[:, woff + kc * 128: woff + (kc + 1) * 128], xk_stride_fn(kc, a, n),
               kc == 0, kc == nkc - 1, [bw] + bxk, [bacc], kc == nkc - 1)

    SQK = [sb(f"SQK{i}", (128, 512), BF16) for i in range(2)]
    RSP = [sb(f"RSP{i}", (128, 512), F32) for i in range(2)]
    bSQK = [Buf(f"SQK{i}") for i in range(2)]
    bRSP = [Buf(f"RSP{i}") for i in range(2)]
    rms_cnt = [0, 0]

    def rmsnorm(sg, gi_gain, out_fn, bout_fn, stack=None):
        pool = AccPool([0, 1])
        for g, (a, n) in enumerate(sg["ngs"]):
            i = rms_cnt[0] % 2
            rms_cnt[0] += 1
            acc, bacc = pool.next()
            for kc in range(NK):
                q = rms_cnt[1] % 2
                rms_cnt[1] += 1
                act(SQK[q][:, 0:n], H[:, kc, a:a + n], AF.Square, [bH[kc][g]], [bSQK[q]])
                mm(acc[:, 0:n], ONES[:, :], SQK[q][:, 0:n], kc == 0, kc == NK - 1, [bSQK[q], bCONST], [bacc], True)
            act(RSP[i][:, 0:n], acc[:, 0:n], AF.Sqrt, [bCONST], [bacc, bRSP[i]], bias=EPSC[:, 0:1], scale=1.0 / D)
            P.op("dve", lambda e, o=RSP[i][:, 0:n]: e.reciprocal(out=o, in_=o), [bRSP[i]], [bRSP[i]])
            for kc in range(NK):
                stt("dve", out_fn(kc, a, n), H[:, kc, a:a + n], GAINS[:, gi_gain, kc:kc + 1], RSP[i][:, 0:n],
                    ALU.mult, ALU.mult, [bH[kc][g], bRSP[i], bCONST], [bout_fn(kc, g)])

    RMSB = [6, 7]

    def rms_accum(sg, g, kc):
        a, n = sg["ngs"][g]
        q = rms_cnt[1] % 2
        rms_cnt[1] += 1
        acc, bacc = PS[RMSB[g]], bPS[RMSB[g]]
        act(SQK[q][:, 0:n], H[:, kc, a:a + n], AF.Square, [bH[kc][g]], [bSQK[q]])
        mm(acc[:, 0:n], ONES[:, :], SQK[q][:, 0:n], kc == 0, kc == NK - 1, [bSQK[q], bCONST], [bacc], True)

    rms_q = []

    def rms_accum_lag(sg, g, kc, lag=RMS_LAG):
        rms_q.append((sg, g, kc))
        while len(rms_q) > lag:
            rms_accum(*rms_q.pop(0))

    def rms_finish(sg, gi_gain, out_fn, bout_fn):
        while rms_q:
            rms_accum(*rms_q.pop(0))
        for g, (a, n) in enumerate(sg["ngs"]):
            i = rms_cnt[0] % 2
            rms_cnt[0] += 1
            acc, bacc = PS[RMSB[g]], bPS[RMSB[g]]
            act(RSP[i][:, 0:n], acc[:, 0:n], AF.Sqrt, [bCONST], [bacc, bRSP[i]], bias=EPSC[:, 0:1], scale=1.0 / D)
            P.op("dve", lambda e, o=RSP[i][:, 0:n]: e.reciprocal(out=o, in_=o), [bRSP[i]], [bRSP[i]])
            for kc in range(NK):
                stt("dve", out_fn(kc, a, n), H[:, kc, a:a + n], GAINS[:, gi_gain, kc:kc + 1], RSP[i][:, 0:n],
                    ALU.mult, ALU.mult, [bH[kc][g], bRSP[i], bCONST], [bout_fn(kc, g)])

    def interleave(task_fns, sets, side=None, side_ratio=1):
        active = []
        free = list(sets)
        it = iter(task_fns)
        pending = next(it, None)
        side_alive = side is not None
        while True:
            if pending is not None and free:
                fn_, ready_ = pending if isinstance(pending, tuple) else (pending, None)
                if ready_ is None or ready_():
                    cs_ = free.pop(0)
                    active.append((fn_(cs_), cs_))
                    pending = next(it, None)
                    if DBG and SLO < P.t['pe'] < SHI: print(f"[sim]   admit task at pe={P.t['pe']:.0f} dve={P.t['dve']:.0f} nactive={len(active)} side_alive={side_alive}")
            if not active and pending is None and not side_alive:
                break
            for ent in list(active):
                if next(ent[0], "END") == "END":
                    active.remove(ent)
                    free.append(ent[1])
                    if DBG and SLO < P.t['pe'] < SHI: print(f"[sim]   task done at pe={P.t['pe']:.0f} dve={P.t['dve']:.0f} act={P.t['act']:.0f}")
            if side_alive:
                nst = 0
                while True:
                    busy_others = max(P.t["dve"], P.t["act"])
                    if POLICY == "fixed":
                        sr_ = side_ratio() if callable(side_ratio) else side_ratio
                        if (active or pending is None) and nst >= sr_ and active:
                            break
                        if not active and pending is not None and nst >= 1:
                            break
                    else:
                        if active and nst >= MINS and P.t["pe"] >= busy_others - MARGIN:
                            break
                        if active and nst >= 6:
                            break
                    if next(side, "END") == "END":
                        side_alive = False
                        break
                    nst += 1

    def run(gen):
        for _ in gen:
            pass

    def load_sg(i_):
        T_ = SGS[i_]["T"]
        for kc in range(NK):
            dma("sp", H[:, kc, 0:T_], x_in[i_][:, kc, :], ch_x, [], [bH[kc][g] for g in range(2)])
        for kc in range(NK):
            for g in range(2):
                bH[kc][g].w = (ch_x, ch_x.count)
                P.fin[(ch_x, ch_x.count)] = max(P.fin.get((ch_x, c_), 0.0) for c_ in range(ch_x.count - NK + 1, ch_x.count + 1))
        for j in range(2):
            dma("sp", CS[:, j, 0:T_], cs_in[i_][:, j, :], ch_cs, [], [bCS])

    for sgi, sg in enumerate(SGS):
        T = sg["T"]
        ngs = sg["ngs"]
        chunks = sg["chunks"]
        nch = len(chunks)
        has_sample = any(k == "sample" for k, _, _ in chunks)
        overlap_conv = not has_sample
        if sgi == 0:
            load_sg(0)
        sg_stack = ExitStack()
        if has_sample:
            SMASK = sb("SMASK", (128, HEADS, 128), F32, sg_stack)
            QDECS = sb("QDECS", (128, HEADS, 128), F32, sg_stack)
            RM3 = sb("RM3", (128, NSS, 128), BF16, sg_stack)
            bSC = Buf("SC")
            dma("sp", SMASK[:], smask_in[:], ch_sc, [], [bSC])
            dma("sp", QDECS[:], qdecs_in[:], ch_sc, [], [bSC])
            dma("pool", RM3[:], rm3_in[:], ch_scsw, [], [bSC])

        for l in range(DEPTH):
            if l == 0:
                if sgi == 0:
                    rmsnorm(sg, l, lambda kc, a, n: A[:, kc, a:a + n], lambda kc, g: bA[kc][g])
            else:
                rms_finish(sg, l, lambda kc, a, n: A[:, kc, a:a + n], lambda kc, g: bA[kc][g])

            conv_stack = ExitStack()

            def conv_alloc():
                cx = {}
                cx["YB"] = sb("YB", (128, NK, T), BF16, conv_stack)
                cx["bYB"] = [Buf(f"YB{g}") for g in range(2)]
                cx["UP"] = [sb(f"UP{i}", (128, 2 + T), F32, conv_stack) for i in range(2)]
                cx["US"] = [sb(f"US{i}", (128, NSS, DS + 2), F32, conv_stack) for i in range(2)] if sgi == 0 else [None, None]
                cx["BG"] = [sb(f"BG{i}", (128, T), F32, conv_stack) for i in range(2)]
                cx["YC"] = [sb(f"YC{i}", (128, T), F32, conv_stack) for i in range(2)]
                cx["CGS"] = [sb(f"CGS{i}", (128, max(n_ for _, n_ in ngs)), F32, conv_stack) for i in range(2)]
                for nm in ("UP", "US", "BG", "YC", "CGS"):
                    cx["b" + nm] = [Buf(f"{nm}{i}") for i in range(2)]
                return cx

            def conv_gen(cx, pool):
                YB, bYB = cx["YB"], cx["bYB"]
                UP, US, BG, YC, CGS = cx["UP"], cx["US"], cx["BG"], cx["YC"], cx["CGS"]
                bUP, bUS, bBG, bYC, bCGS = cx["bUP"], cx["bUS"], cx["bBG"], cx["bYC"], cx["bCGS"]
                plen, pstart = sg["plen"], sg["pstart"]
                it = 0
                for c in range(NK):
                    w, bw = w_next("conv")
                    u = c % 2
                    if sgi == 0:
                        P.op("dve", lambda e, u=u: e.memset(UP[u][:, 0:2], 0.0), [], [bUP[u]])
                        cp("dve", US[u][:, :, 0:2], CPV[:, l, c, :, :], [bCONST], [bUS[u]])
                    else:
                        cp("dve", UP[u][:, 0:2], CARRY[:, l, c, :], [bCARRY[l][c]], [bUP[u]])
                    for g, (a, n) in enumerate(ngs):
                        i = it % 2
                        it += 1
                        acc1, bacc1 = pool.next()
                        acc2, bacc2 = pool.next()
                        acc3, bacc3 = pool.next()
                        xa = lambda kc, a, n: A[:, kc, a:a + n]
                        dense_acc(acc1, bacc1, w, bw, 0, NK, A, xa, a, n, AK(g))
                        dense_acc(acc2, bacc2, w, bw, 1024, NK, A, xa, a, n, AK(g))
                        dense_acc(acc3, bacc3, w, bw, 2048, NK, A, xa, a, n, AK(g))
                        cp("act", CGS[i][:, 0:n], acc1[:, 0:n], [], [bacc1, bCGS[i]])
                        if sgi == 0 and g == 0:
                            tt("dve", US[u][:, :, 2:2 + DS], acc2[:, 0:128].rearrange("p (s i) -> p s i", s=NSS),
                               CGS[i][:, 0:128].rearrange("p (s i) -> p s i", s=NSS), ALU.mult, [bCGS[i]], [bacc2, bUS[u]])
                            tt("dve", UP[u][:, 2:2 + 16], acc2[:, 128:144], CGS[i][:, 128:144], ALU.mult, [bCGS[i]], [bacc2, bUP[u]])
                        else:
                            o0 = 2 + (a - pstart)
                            tt("dve", UP[u][:, o0:o0 + n], acc2[:, 0:n], CGS[i][:, 0:n], ALU.mult, [bCGS[i]], [bacc2, bUP[u]])
                        cp("act", BG[u][:, a:a + n], acc3[:, 0:n], [], [bacc3, bBG[u]])
                        yield
                    yp = YC[u][:, pstart:pstart + plen]
                    act(yp, UP[u][:, 0:plen], AF.Identity, [bUP[u], bCONST], [bYC[u]], scale=CONVW[:, l, c, 0:1])
                    stt("dve", yp, UP[u][:, 1:1 + plen], CONVW[:, l, c, 1:2], yp, ALU.mult, ALU.add, [bUP[u], bYC[u], bCONST], [bYC[u]])
                    stt("dve", yp, UP[u][:, 2:2 + plen], CONVW[:, l, c, 2:3], yp, ALU.mult, ALU.add, [bUP[u], bYC[u], bCONST], [bYC[u]])
                    if sgi == 0:
                        ys = YC[u][:, 0:128].rearrange("p (s i) -> p s i", s=NSS)
                        ts("dve", ys, US[u][:, :, 0:DS], CONVW[:, l, c, 0:1], None, ALU.mult, None, [bUS[u], bCONST], [bYC[u]])
                        stt("dve", ys, US[u][:, :, 1:1 + DS], CONVW[:, l, c, 1:2], ys, ALU.mult, ALU.add, [bUS[u], bYC[u], bCONST], [bYC[u]])
                        stt("dve", ys, US[u][:, :, 2:2 + DS], CONVW[:, l, c, 2:3], ys, ALU.mult, ALU.add, [bUS[u], bYC[u], bCONST], [bYC[u]])
                        cp("dve", CSS[:, l, c, :, :], US[u][:, :, DS:DS + 2], [bUS[u]], [bCSS[l][c]])
                    cp("dve", CARRY[:, l, c, :], UP[u][:, plen:plen + 2], [bUP[u]], [bCARRY[l][c]])
                    yield
                    tt(POOL_ENG, YB[:, c, 0:T], BG[u][:, 0:T], YC[u][:, 0:T], ALU.mult, [bBG[u], bYC[u]], [bYB[0], bYB[1]])
                    yield

            cx = conv_alloc() if overlap_conv else None

            with ExitStack() as st:
                QT = [sb(f"QT{hh}", (128, 4, T), BF16, st) for hh in range(2)]
                KT = [sb(f"KT{hh}", (128, 4, T), BF16, st) for hh in range(2)]
                V = [sb(f"V{hh}", (128, nch, 512), BF16, st) for hh in range(2)]
                RA = [sb(f"RA{i}", (128, 512), F32, st) for i in range(2)]
                RB = [sb(f"RB{i}", (128, 512), F32, st) for i in range(2)]
                bQT = [[Buf(f"QT{hh}_{g}") for g in range(2)] for hh in range(2)]
                bKT = [[Buf(f"KT{hh}_{g}") for g in range(2)] for hh in range(2)]
                bV = [[Buf(f"V{hh}_{c}") for c in range(nch)] for hh in range(2)]
                bRA = [Buf(f"RA{i}") for i in range(2)]
                bRB = [Buf(f"RB{i}") for i in range(2)]
                NSET = NSET_S if has_sample else 4
                set_banks = [4, 5, 6, 7, 3, 2][:NSET]
                csets = []
                for si_ in range(NSET):
                    d_ = dict(ps=PS[set_banks[si_]], bps=bPS[set_banks[si_]])
                    for nm, shp, dt_ in (("PTm", (128, 4, 128), BF16), ("QTL", (128, 4, 128), BF16),
                                         ("ON", (128, 4, 128), BF16), ("KH", (128, 4, 128), BF16), ("STAT", (128, 8, 4), F32)):
                        d_[nm] = sb(f"{nm}s{si_}", shp, dt_, st)
                        d_["b" + nm] = Buf(f"{nm}s{si_}")
                    csets.append(d_)
                SQOs = [sb(f"SQO{i}", (128, 4, 128), F32, st) for i in range(2)]
                bSQOs = [Buf(f"SQO{i}") for i in range(2)]
                sqo_i = [0]
                if has_sample:
                    QM = sb("QM", (128, NSS, 128), BF16, st)
                    KM = sb("KM", (128, NSS, 128), BF16, st)
                    SINQ = [sb(f"SINQ{i}", (128, 4, 128), F32, st) for i in range(4)]
                    SINBFH = [sb(f"SINBFH{i}", (128, 8, 128), BF16, st) for i in range(2)]
                    bSINQ = [Buf(f"SINQ{i}") for i in range(4)]
                    bSINBFH = [Buf(f"SINBFH{i}") for i in range(2)]
                    bQM, bKM = Buf("QM"), Buf("KM")
                    P.op("dve", lambda e: e.memset(QM[:], 0.0), [], [bQM])
                    P.barrier(engines=("sp",))
                rot_i = [0]

                def rotary(acc, bacc, dst, bdst, a, n):
                    i = rot_i[0] % 2
                    rot_i[0] += 1
                    tt("dve", RA[i][:, 0:n], acc[:, 0:n], CS[:, 0, a:a + n], ALU.mult, [bCS], [bacc, bRA[i]])
                    tt("dve", RB[i][0:64, 0:n], acc[64:128, 0:n], CS[64:128, 1, a:a + n], ALU.mult, [bCS], [bacc, bRB[i]])
                    tt("dve", RB[i][64:128, 0:n], acc[0:64, 0:n], CS[0:64, 1, a:a + n], ALU.mult, [bCS], [bacc, bRB[i]])
                    tt(POOL_ENG, dst, RA[i][:, 0:n], RB[i][:, 0:n], ALU.add, [bRA[i], bRB[i]], [bdst])

                dpool = AccPool([b_ for b_ in range(4) if b_ not in set_banks])

                def dense_gen(hh, parts=("v", "q", "k", "g")):
                    h0 = hh * 4
                    pool = dpool
                    xa = lambda kc, a, n: A[:, kc, a:a + n]
                    for part in parts:
                        if part == "v":
                            wv, bwv = w_next("wv")
                            for ci, (kind, t0, L) in enumerate(chunks):
                                g = ng_of(sg, t0)
                                acc, bacc = pool.next()
                                for kc in range(NK):
                                    mm(acc[0:L, 0:512], A[:, kc, t0:t0 + L], wv[:, kc * 512:(kc + 1) * 512], kc == 0, kc == NK - 1,
                                       [bwv, bA[kc][g]], [bacc], kc == NK - 1)
                                cp("act", V[hh][0:L, ci, :], acc[0:L, 0:512], [], [bacc, bV[hh][ci]])
                                vdone[hh] = ci + 1
                                yield
                        elif part in ("q", "k"):
                            w_, bw_ = w_next("w" + part)
                            dstT, bdstT = (QT, bQT) if part == "q" else (KT, bKT)
                            for j in range(4):
                                for g, (a, n) in enumerate(ngs):
                                    acc, bacc = pool.next()
                                    dense_acc(acc, bacc, w_, bw_, j * 1024, NK, A, xa, a, n, AK(g))
                                    rotary(acc, bacc, dstT[hh][:, j, a:a + n], bdstT[hh][g], a, n)
                                    yield
                        else:
                            wg_, bwg = w_next("wg")
                            for j in range(4):
                                for g, (a, n) in enumerate(ngs):
                                    acc, bacc = pool.next()
                                    dense_acc(acc, bacc, wg_, bwg, j * 1024, NK, A, xa, a, n, AK(g))
                                    act(O[:, h0 + j, a:a + n], acc[:, 0:n], AF.Silu, [], [bacc, bO[h0 + j][g]])
                                    yield

                def chunk_gen(hh, ci, cs_, part="all"):
                    kind, t0, L = chunks[ci]
                    h0 = hh * 4
                    g = ng_of(sg, t0)
                    ps_, bps = cs_["ps"], cs_["bps"]
                    bpt = bps
                    PTm, QTL, ON, KH, STAT = (cs_[k_] for k_ in ("PTm", "QTL", "ON", "KH", "STAT"))
                    bPTm, bQTL, bON, bKH, bSTAT = (cs_["b" + k_] for k_ in ("PTm", "QTL", "ON", "KH", "STAT"))
                    SQO, bSQO = SQOs[sqo_i[0] % 2], bSQOs[sqo_i[0] % 2]
                    sqo_i[0] += 1
                    ps3 = ps_[:].rearrange("p (j i) -> p j i", j=4)
                    ps_bf = ps_[:, 0:256].bitcast(BF16)
                    ps_k = ps_bf
                    ps_t = ps_bf
                    ps_k3 = ps_k.rearrange("p (j i) -> p j i", j=4)
                    ps_t3 = ps_t.rearrange("p (j i) -> p j i", j=4)
                    qt, kt, v = QT[hh], KT[hh], V[hh]
                    bqt, bkt, bv = bQT[hh][g], bKT[hh][g], bV[hh][ci]
                    kd = {"full": 0, "lead": 1, "sample": 2}[kind]
                    cdeps = [bCONST] + ([bSC] if kind == "sample" else [])
                    mask = SMASK if kind == "sample" else DMASK
                    if MERGE01:
                        pk_, bpk_ = dpool.next()
                        ps_k = pk_[:, 0:256].bitcast(BF16)
                        ps_k3 = ps_k.rearrange("p (j i) -> p j i", j=4)
                    else:
                        bpk_ = bpt
                    for j in range(4):
                        tr(ps_k[0:L, j * 128:(j + 1) * 128], kt[:, j, t0:t0 + L], IDENT[:, :], [bkt, bCONST], [bpk_], j == 3)
                    for j in range(4):
                        act(KH[0:L, j, :], ps_k3[0:L, j, :], AF.Identity, [bCONST], [bpk_, bKH], scale=KDEC[0:L, kd, h0 + j:h0 + j + 1])
                    if not MERGE01:
                        yield
                    if part == "state":
                        yield
                    if part != "state":
                        for j in range(4):
                            mm(ps_[0:L, j * 128:j * 128 + L], kt[:, j, t0:t0 + L], qt[:, j, t0:t0 + L], True, True,
                               [bkt, bqt], [bps], j == 3)
                        tt("dve", PTm[0:L, :, 0:L], ps3[0:L, :, 0:L], mask[0:L, h0:h0 + 4, 0:L], ALU.mult, cdeps, [bps, bPTm])
                        if kind == "full":
                            tt(POOL_ENG, QTL[:, :, 0:L], qt[:, :, t0:t0 + L], QDEC[:, h0:h0 + 4, 0:L], ALU.mult, [bqt, bCONST], [bQTL])
                        elif kind == "sample":
                            tt("dve", QTL[:, :, 0:L], qt[:, :, t0:t0 + L], QDECS[:, h0:h0 + 4, 0:L], ALU.mult, [bqt] + cdeps, [bQTL])
                        yield
                        assert vdone[hh] > ci, "o-matmul stage emitted before V of this chunk"
                        if kind == "sample":
                            pstep = QM[:].ap[0][0]
                            def ld_bf(jj, hf):
                                dma("pool", SINBFH[hf][:], sret_in[l][:, h0 + jj, hf * 8:(hf + 1) * 8, :], ch_sinbf[hf], [], [bSINBFH[hf]])
                            ld_bf(0, 0)
                            ld_bf(0, 1)
                            for j in range(4):
                                h = h0 + j
                                qm_diag = bass.AP(QM, 0, [[pstep, 128], [136, NSS], [1, DS]])
                                q_src = QTL[:, j, :].rearrange("p (s i) -> p s i", s=NSS)
                                cp("dve", qm_diag, q_src, [bQTL], [bQM])
                                mm(ps_[:, j * 128:(j + 1) * 128], PTm[:, j, :], v[:, ci, j * 128:(j + 1) * 128], True, False,
                                   [bPTm, bv], [bps], False)
                                for s in range(NSS):
                                    mm(ps_[:, j * 128:(j + 1) * 128], QM[:, s, :], SINBFH[s // 8][:, s % 8, :], False, s == NSS - 1,
                                       [bQM, bSINBFH[s // 8]], [bps], s in (7, NSS - 1))
                                    if s in (7, NSS - 1) and j < 3:
                                        ld_bf(j + 1, s // 8)
                                yield
                        else:
                            first = kind == "lead"
                            for j in range(4):
                                h = h0 + j
                                mm(ps_[0:L, j * 128:(j + 1) * 128], PTm[0:L, j, 0:L], v[0:L, ci, j * 128:(j + 1) * 128], True, first,
                                   [bPTm, bv], [bps], first and j == 3)
                                if not first:
                                    mm(ps_[0:L, j * 128:(j + 1) * 128], QTL[:, j, 0:L], SBF[:, l, h * 128:(h + 1) * 128], False, True,
                                       [bQTL, bSBF[l][hh]], [bps], j == 3)
                        if kind != "sample":
                            psS, bpsS = dpool.next()
                            for j in range(4):
                                mm(psS[:, j * 128:(j + 1) * 128], KH[0:L, j, :], v[0:L, ci, j * 128:(j + 1) * 128], True, True,
                                   [bKH, bv], [bpsS], j == 3)
                            if kind == "lead":
                                cp("dve", SST[:, l, h0 * 128:(h0 + 4) * 128], psS[:, :], [], [bpsS, bSST[l][hh]])
                            else:
                                for j in range(4):
                                    gLj = float(np.exp(LOGG[h0 + j] * 128.0))
                                    sl = SST[:, l, (h0 + j) * 128:(h0 + j + 1) * 128]
                                    stt("dve", SBF[:, l, (h0 + j) * 128:(h0 + j + 1) * 128], sl, gLj, psS[:, j * 128:(j + 1) * 128],
                                        ALU.mult, ALU.add, [bSST[l][hh]], [bpsS, bSBF[l][hh]])
                                for j in range(4):
                                    gLj = float(np.exp(LOGG[h0 + j] * 128.0))
                                    sl = SST[:, l, (h0 + j) * 128:(h0 + j + 1) * 128]
                                    stt("dve", sl, sl, gLj, psS[:, j * 128:(j + 1) * 128], ALU.mult, ALU.add,
                                        [bSST[l][hh]], [bpsS, bSST[l][hh]])
                            if kind == "lead":
                                cp("act", SBF[:, l, h0 * 128:(h0 + 4) * 128], SST[:, l, h0 * 128:(h0 + 4) * 128], [bSST[l][hh]], [bSBF[l][hh]])
                            yield
                        act(SQO[0:L, :, :], ps3[0:L, :, :], AF.Square, [], [bps, bSQO])
                        P.op("dve", lambda e: e.tensor_reduce(out=STAT[0:L, 0, :], in_=ps3[0:L, :, :], axis=AX.X, op=ALU.add),
                             [], [bps, bSTAT])
                        P.op("dve", lambda e: e.tensor_reduce(out=STAT[0:L, 1, :], in_=SQO[0:L, :, :], axis=AX.X, op=ALU.add),
                             [bSQO], [bSTAT])
                        if not MERGE3:
                            yield
                        ts("dve", STAT[0:L, 2, :], STAT[0:L, 0, :], 1.0 / 128, None, ALU.mult, None, [bSTAT], [bSTAT])
                        tt("dve", STAT[0:L, 3, :], STAT[0:L, 2, :], STAT[0:L, 2, :], ALU.mult, [bSTAT], [bSTAT])
                        stt("dve", STAT[0:L, 4, :], STAT[0:L, 1, :], 1.0 / 128, STAT[0:L, 3, :], ALU.mult, ALU.subtract, [bSTAT], [bSTAT])
                        act(STAT[0:L, 5, :], STAT[0:L, 4, :], AF.Sqrt, [bSTAT, bCONST], [bSTAT], bias=EPSC[0:L, 0:1], scale=1.0)
                        P.op("dve", lambda e: e.reciprocal(out=STAT[0:L, 6, :], in_=STAT[0:L, 5, :]), [bSTAT], [bSTAT])
                        stt("dve", STAT[0:L, 7, :], STAT[0:L, 2, :], -1.0, STAT[0:L, 6, :], ALU.mult, ALU.mult, [bSTAT], [bSTAT])
                        yield
                        for j in range(4):
                            act(ON[0:L, j, :], ps3[0:L, j, :], AF.Identity, [bSTAT], [bps, bON],
                                bias=STAT[0:L, 7, j:j + 1], scale=STAT[0:L, 6, j:j + 1])
                        yield
                        for j in range(4):
                            tr(ps_t[:, j * 128:j * 128 + L], ON[0:L, j, :], IDENT[0:L, 0:L], [bON, bCONST], [bpt], j == 3)
                        yield
                        ob = [bO[h0 + j][g] for j in range(4)]
                        assert flags["g%d" % hh], "gate stage emitted before silu(g) of this half"
                        tt("dve", O[:, h0:h0 + 4, t0:t0 + L], ps_t3[:, :, 0:L], O[:, h0:h0 + 4, t0:t0 + L], ALU.mult, ob, [bpt] + ob)
                    if kind == "sample" and part != "o":
                        def ld_q(k):
                            dma("sp", SINQ[k % 4][:], sret_in[l][:, h0 + k // 4, (k % 4) * 4:(k % 4 + 1) * 4, :], ch_sin[k % 4], [], [bSINQ[k % 4]])
                        ld_q(0)
                        ld_q(1)
                        for k in range(16):
                            j, sq = k // 4, k % 4
                            h = h0 + j
                            if sq == 0:
                                kh_b = bass.AP(KH, KH[:, j, :].offset, [[KH[:].ap[0][0], 128], [0, NSS], [1, 128]])
                                tt("dve", KM[:], kh_b, RM3[:], ALU.mult, [bKH, bSC], [bKM])
                            if k + 2 < 16:
                                ld_q(k + 2)
                            g8 = float(np.exp(LOGG[h] * DS))
                            psq, bpsq = dpool.next()
                            for si in range(4):
                                s = sq * 4 + si
                                mm(psq[:, si * 128:(si + 1) * 128], KM[:, s, :], v[:, ci, j * 128:(j + 1) * 128], True, True,
                                   [bKM, bv], [bpsq], si == 3)
                            stt("dve", SINQ[sq][:], SINQ[sq][:], g8, psq[:].rearrange("p (s e) -> p s e", s=4), ALU.mult, ALU.add,
                                [bSINQ[sq]], [bpsq, bSINQ[sq]])
                            dma("sp", oss[l][:, h, sq * 4:(sq + 1) * 4, :], SINQ[sq][:], ch_sout[sq], [bSINQ[sq]], [])
                            if k % 2 == 1:
                                yield

                if DBG: print(f"[sim] sg{sgi} l{l} P1 start pe={P.t['pe']:.0f} dve={P.t['dve']:.0f}"); b0_ = dict(P.busy); t0_ = P.t['pe']
                vdone = [0, 0]
                flags = {"d1": False, "g0": False, "kq1": False, "g1": False}
                early = SG0_EARLY and has_sample
                run(dense_gen(0, ("k", "q")))
                if early:
                    run(dense_gen(1, ("k", "q")))
                    flags["kq1"] = True

                def side_chain():
                    if early:
                        for hh_ in range(2):
                            for _ in dense_gen(hh_, ("v",)):
                                yield
                        for hh_ in range(2):
                            for _ in dense_gen(hh_, ("g",)):
                                yield
                            flags["g%d" % hh_] = True
                        flags["d1"] = True
                        return
                    for _ in dense_gen(0, ("v",)):
                        yield
                    for _ in dense_gen(0, ("g",)):
                        yield
                    flags["g0"] = True
                    for _ in dense_gen(1, ("k", "q")):
                        yield
                    flags["kq1"] = True
                    for _ in dense_gen(1, ("v",)):
                        yield
                    for _ in dense_gen(1, ("g",)):
                        yield
                    flags["g1"] = True
                    flags["d1"] = True
                    if overlap_conv:
                        for _ in conv_gen(cx, dpool):
                            yield

                tasks_ = []
                order_ = [(hh_, ci) for ci in range(nch) for hh_ in range(2)] if early else \
                         [(hh_, ci) for hh_ in range(2) for ci in range(nch)]
                for hh_, ci in order_:
                    rdy = None if (hh_ == 0 or early) else (lambda: flags["kq1"])
                    if chunks[ci][0] == "sample" and SPLIT_SAMPLE:
                        tasks_.append(((lambda cs_, ci=ci, hh_=hh_: chunk_gen(hh_, ci, cs_, "o")), rdy))
                        tasks_.append(((lambda cs_, ci=ci, hh_=hh_: chunk_gen(hh_, ci, cs_, "state")), rdy))
                    else:
                        tasks_.append(((lambda cs_, ci=ci, hh_=hh_: chunk_gen(hh_, ci, cs_)), rdy))
                if early:
                    sr_fn = lambda: SIDE_RATIO + (2 if not flags["g1"] else 0)
                else:
                    sr_fn = lambda: SIDE_RATIO + (1 if (not flags["g0"] or (flags["kq1"] and not flags["g1"])) else 0)
                interleave(tasks_, csets, side=side_chain(), side_ratio=sr_fn)
                if sgi == 2:
                    dma("sp", osp[l], SST[:, l, :], ch_small, [bSST[l][0], bSST[l][1]], [])
                if DBG: print(f"[sim] sg{sgi} l{l} P1 done pe={P.t['pe']:.0f} dve={P.t['dve']:.0f} act={P.t['act']:.0f}", "dur", round(P.t['pe'] - t0_), "busy", {k_: round(P.busy[k_] - b0_.get(k_, 0)) for k_ in P.busy})
            if has_sample:
                P.barrier()

            if not overlap_conv:
                cx = conv_alloc()
                run(conv_gen(cx, AccPool([0, 1, 2, 3, 4, 5, 6, 7])))
            YB, bYB = cx["YB"], cx["bYB"]

            with ExitStack() as st2:
                R = sb("R", (128, NK, T), BF16, st2)
                bR = [Buf(f"R{g}") for g in range(2)]
                SG_ = [sb(f"SGa{i}", (128, 512), F32, st2) for i in range(2)]
                T1 = [sb(f"T1{i}", (128, 512), F32, st2) for i in range(2)]
                bSG = [Buf("SGa0"), Buf("SGa1")]
                bT1 = [Buf("T10"), Buf("T11")]
                pool = AccPool([0, 1, 2, 3, 4, 5])
                it = 0
                for mp in range(4):
                    w, bw = w_next("garo")
                    for mi in range(2):
                        m = 2 * mp + mi
                        for g, (a, n) in enumerate(ngs):
                            i = it % 2
                            it += 1
                            acc1, bacc1 = pool.next()
                            acc2, bacc2 = pool.next()
                            dense_acc(acc1, bacc1, w, bw, mi * 2048, NK, A, lambda kc, a, n: A[:, kc, a:a + n], a, n, AK(g))
                            dense_acc(acc2, bacc2, w, bw, mi * 2048 + 1024, NK, O, lambda kc, a, n: O[:, kc, a:a + n], a, n,
                                      [bO[h][g] for h in range(HEADS)])
                            act(SG_[i][:, 0:n], acc1[:, 0:n], AF.Sigmoid, [], [bacc1, bSG[i]])
                            tt("dve", R[:, m, a:a + n], acc2[:, 0:n], SG_[i][:, 0:n], ALU.mult, [bSG[i]], [bacc2, bR[g]])
                for mp in range(4):
                    w, bw = w_next("gbco")
                    for mi in range(2):
                        m = 2 * mp + mi
                        for g, (a, n) in enumerate(ngs):
                            i = it % 2
                            it += 1
                            acc1, bacc1 = pool.next()
                            acc2, bacc2 = pool.next()
                            dense_acc(acc1, bacc1, w, bw, mi * 2048, NK, A, lambda kc, a, n: A[:, kc, a:a + n], a, n, AK(g))
                            dense_acc(acc2, bacc2, w, bw, mi * 2048 + 1024, NK, YB, lambda kc, a, n: YB[:, kc, a:a + n], a, n, [bYB[g]])
                            act(SG_[i][:, 0:n], acc1[:, 0:n], AF.Sigmoid, [], [bacc1, bSG[i]])
                            tt("dve", T1[i][:, 0:n], acc2[:, 0:n], SG_[i][:, 0:n], ALU.mult, [bSG[i]], [bacc2, bT1[i]])
                            tt("dve", R[:, m, a:a + n], T1[i][:, 0:n], R[:, m, a:a + n], ALU.add, [bT1[i], bR[g]], [bR[g]])
                for wi in range(2):
                    w, bw = w_next("wo")
                    for mj in range(4):
                        m = wi * 4 + mj
                        for g, (a, n) in enumerate(ngs):
                            acc, bacc = pool.next()
                            dense_acc(acc, bacc, w, bw, mj * 1024, NK, R, lambda kc, a, n: R[:, kc, a:a + n], a, n, [bR[g]])
                            tt("dve", H[:, m, a:a + n], acc[:, 0:n], H[:, m, a:a + n], ALU.add, [bH[m][g]], [bacc, bH[m][g]])
                            rms_accum_lag(sg, g, m)
                rms_finish(sg, 2 + l, lambda kc, a, n: A[:, kc, a:a + n], lambda kc, g: bA[kc][g])
            conv_stack.close()

            if DBG: print(f"[sim] sg{sgi} l{l} P5/P6 done pe={P.t['pe']:.0f} dve={P.t['dve']:.0f}")
            with ExitStack() as st2:
                AB = sb("AB", (128, NF, T), BF16, st2)
                bAB = [Buf(f"AB{g}") for g in range(2)]
                SL = [sb(f"SL{i}", (128, 512), F32, st2) for i in range(2)]
                bSL = [Buf("SL0"), Buf("SL1")]
                pool = AccPool([0, 1, 2, 3, 4, 5])
                it = 0
                for fp in range(11):
                    w, bw = w_next("ffn1")
                    for g, (a, n) in enumerate(ngs):
                        for fi in range(2):
                            f = 2 * fp + fi
                            i = it % 2
                            it += 1
                            acc1, bacc1 = pool.next()
                            acc2, bacc2 = pool.next()
                            xa = lambda kc, a, n: A[:, kc, a:a + n]
                            dense_acc(acc1, bacc1, w, bw, fi * 2048, NK, A, xa, a, n, AK(g))
                            dense_acc(acc2, bacc2, w, bw, fi * 2048 + 1024, NK, A, xa, a, n, AK(g))
                            act(SL[i][:, 0:n], acc1[:, 0:n], AF.Silu, [], [bacc1, bSL[i]])
                            tt("dve", AB[:, f, a:a + n], acc2[:, 0:n], SL[i][:, 0:n], ALU.mult, [bSL[i]], [bacc2, bAB[g]])
                for m in range(NK):
                    w, bw = w_next("wd")
                    for g, (a, n) in enumerate(ngs):
                        acc, bacc = pool.next()
                        dense_acc(acc, bacc, w, bw, 0, NF, AB, lambda kc, a, n: AB[:, kc, a:a + n], a, n, [bAB[g]])
                        tt("dve", H[:, m, a:a + n], acc[:, 0:n], H[:, m, a:a + n], ALU.add, [bH[m][g]], [bacc, bH[m][g]])
                        rms_accum_lag(sg, g, m)

        sg_stack.close()
        if has_sample:
            P.barrier()
        with ExitStack() as st2:
            Y = sb("Y", (128, NK, T), F32, st2)
            bY = [[Buf(f"Y{k}_{g}") for g in range(2)] for k in range(NK)]
            rms_finish(sg, 4, lambda kc, a, n: Y[:, kc, a:a + n], lambda kc, g: bY[kc][g])
            for kc in range(NK):
                dma("sp", y_out[sgi][:, kc, :], Y[:, kc, 0:T], ch_y, [bY[kc][g] for g in range(len(ngs))], [])
            if sgi == 2:
                dma("sp", ocp[:], CARRY[:], ch_small, [b for l in range(DEPTH) for b in bCARRY[l]], [])
            if sgi == 0:
                dma("sp", ocs[:], CSS[:], ch_small, [b for l in range(DEPTH) for b in bCSS[l]], [])
            if sgi + 1 < len(SGS):
                load_sg(sgi + 1)
                rmsnorm(SGS[sgi + 1], 0, lambda kc, a, n: A[:, kc, a:a + n], lambda kc, g: bA[kc][g])
            P.barrier()

    assert wstate["cur"] == len(wseq) - 1
    print("[sim] engine clocks (us):", {k_: round(v_, 1) for k_, v_ in P.t.items()}, "busy", {k_: round(v_, 1) for k_, v_ in P.busy.items()})
    P.barrier(engines=("sp",))

    for s in list(P.src.values()) + P.chans:
        s.sem = es.enter_context(nc.semaphore(s.name))
    block = es.enter_context(nc.Block())
    P.emit(block)
    es.close()
    return nc


LOGG = np.log1p(-np.exp2(-5.0 - np.arange(HEADS, dtype=np.float64)))


def const_tables():
    f32 = np.float32
    scale = 128.0 ** -0.5
    i = np.arange(128)
    diff = i[None, :] - i[:, None]
    dmask = np.zeros((128, HEADS, 128), np.float64)
    smask = np.zeros((128, HEADS, 128), np.float64)
    same = (i[None, :] // DS) == (i[:, None] // DS)
    for h in range(HEADS):
        dec = np.where(diff >= 0, np.exp(LOGG[h] * np.maximum(diff, 0)), 0.0) * scale
        dmask[:, h, :] = dec
        smask[:, h, :] = np.where(same, dec, 0.0)
    qdec = np.zeros((128, HEADS, 128))
    qdecs = np.zeros((128, HEADS, 128))
    kdec = np.zeros((128, 3, HEADS))
    for h in range(HEADS):
        qdec[:, h, :] = np.exp(LOGG[h] * (i + 1.0))[None, :]
        qdecs[:, h, :] = np.exp(LOGG[h] * ((i % DS) + 1.0))[None, :]
        kdec[:, 0, h] = np.exp(LOGG[h] * (127.0 - i)) * scale
        kdec[:, 1, h] = np.exp(LOGG[h] * (15.0 - np.minimum(i, 15))) * scale
        kdec[:, 2, h] = np.exp(LOGG[h] * (DS - 1.0 - (i % DS))) * scale
    rm3 = np.zeros((128, NSS, 128))
    for s in range(NSS):
        rm3[s * DS:(s + 1) * DS, s, :] = 1.0
    inv = np.power(np.float32(10000.0), -np.arange(64, dtype=f32) / np.float32(64)).astype(f32)
    cs = []
    for sg in SGS:
        T = sg["T"]
        pos = np.zeros(T, f32)
        if sg["p0"] is None:
            pos[0:128] = PAST + (np.arange(128) % DS)
            pos[128:144] = np.arange(16)
            pos[144:] = 16 + np.arange(T - 144)
        else:
            pos[:] = N_META + sg["p0"] + np.arange(T)
        ang = (pos[:, None] * inv[None, :]).astype(f32)
        cos, sin = np.cos(ang).astype(f32).T, np.sin(ang).astype(f32).T
        tab = np.zeros((128, 2, T), f32)
        tab[0:64, 0], tab[64:128, 0] = cos, cos
        tab[0:64, 1], tab[64:128, 1] = sin, -sin
        cs.append(tab)
    return dict(dmask=dmask.astype(f32), smask=smask.astype(f32), qdec=qdec.astype(f32), qdecs=qdecs.astype(f32),
                kdec=kdec.astype(f32), rm3=rm3.astype(f32), ident=np.eye(128, dtype=f32)), cs


_CACHE = {}


def kernel(x_prompt, x_sample, state_ret, state_conv, meta_tokens, norm_mix_g, w_in, conv_w,
           w_ret_o, w_conv_o, w_o, norm_ffn_g, w_gate_up, w_down, final_norm_g):
    f32 = np.float32
    A_ = lambda v: np.asarray(v, dtype=f32)
    x_prompt, x_sample, state_ret, state_conv, meta_tokens = map(A_, (x_prompt, x_sample, state_ret, state_conv, meta_tokens))
    w_in, w_ret_o, w_conv_o, w_o, w_gate_up, w_down = map(A_, (w_in, w_ret_o, w_conv_o, w_o, w_gate_up, w_down))
    norm_mix_g, norm_ffn_g, final_norm_g, conv_w = map(A_, (norm_mix_g, norm_ffn_g, final_norm_g, conv_w))

    if "nc" not in _CACHE:
        _CACHE["nc"] = build_program()
    nc = _CACHE["nc"]

    consts, cs = const_tables()
    wpk = pack_weights(w_in, w_ret_o, w_conv_o, w_o, w_gate_up, w_down)
    gains = np.stack([norm_mix_g[0], norm_mix_g[1], norm_ffn_g[0], norm_ffn_g[1], final_norm_g], 0)
    gains = np.ascontiguousarray(gains.reshape(5, NK, 128).transpose(2, 0, 1))
    convw = np.ascontiguousarray(conv_w.reshape(DEPTH, 3, NK, 128).transpose(3, 0, 2, 1))

    def fm(X):
        return np.ascontiguousarray(X.T.reshape(NK, 128, X.shape[0]).transpose(1, 0, 2))

    in_maps = []
    for c in range(NCORES):
        xs = x_sample[c * NSS:(c + 1) * NSS].reshape(NSS * DS, D)
        X0 = np.concatenate([xs, meta_tokens, x_prompt[c, 0:512]], 0)
        X1 = x_prompt[c, 512:1280]
        X2 = x_prompt[c, 1280:2048]
        sret = np.ascontiguousarray(state_ret[:, c * NSS:(c + 1) * NSS].transpose(0, 3, 2, 1, 4))
        sconv = np.ascontiguousarray(state_conv[:, c * NSS:(c + 1) * NSS].reshape(DEPTH, NSS, 2, NK, 128).transpose(4, 0, 3, 1, 2))
        m = dict(x0=fm(X0), x1=fm(X1), x2=fm(X2), cs0=cs[0], cs1=cs[1], cs2=cs[2], wpk=wpk,
                 gains=gains, convw=convw, sret=sret, sconv=sconv)
        m.update(consts)
        in_maps.append(m)

    res = run_bass_kernel_spmd(nc, in_maps, core_ids=list(range(NCORES)))
    R = res.results

    y_prompt = np.empty((NCORES, SEQ, D), f32)
    y_sample = np.empty((NCORES * NSS, DS, D), f32)
    s_p = np.empty((DEPTH, NCORES, HEADS, 128, 128), f32)
    c_p = np.empty((DEPTH, NCORES, 2, D), f32)
    s_s = np.empty((DEPTH, NCORES * NSS, HEADS, 128, 128), f32)
    c_s = np.empty((DEPTH, NCORES * NSS, 2, D), f32)

    def tm(Y):
        return Y.transpose(2, 1, 0).reshape(Y.shape[2], D)

    for c in range(NCORES):
        r = R[c]
        Y0, Y1, Y2 = tm(r["y0"]), tm(r["y1"]), tm(r["y2"])
        y_sample[c * NSS:(c + 1) * NSS] = Y0[0:128].reshape(NSS, DS, D)
        y_prompt[c, 0:512] = Y0[144:656]
        y_prompt[c, 512:1280] = Y1
        y_prompt[c, 1280:2048] = Y2
        s_p[:, c] = r["osp"].reshape(DEPTH, 128, HEADS, 128).transpose(0, 2, 1, 3)
        c_p[:, c] = r["ocp"].transpose(1, 3, 2, 0).reshape(DEPTH, 2, D)
        s_s[:, c * NSS:(c + 1) * NSS] = r["oss"].transpose(0, 3, 2, 1, 4)
        c_s[:, c * NSS:(c + 1) * NSS] = r["ocs"].transpose(1, 3, 4, 2, 0).reshape(DEPTH, NSS, 2, D)
    return (y_prompt, y_sample, s_p, c_p, s_s, c_s)
```

```python
import numpy as np
from contextlib import ExitStack
import concourse.bass as bass
import concourse.mybir as mybir
from concourse.bass_utils import run_bass_kernel_spmd

F32 = mybir.dt.float32
BF16 = mybir.dt.bfloat16
AF = mybir.ActivationFunctionType
ALU = mybir.AluOpType
AX = mybir.AxisListType

D = 1024
NK = 8
HEADS = 8
DFF = 2816
NF = 22
N_META = 16
SEQ = 2048
PAST = 16384
NCORES = 8
NSS = 16
DS = 8
EPS = 1e-6
DEPTH = 2

SGS = [
    dict(T=656, ngs=[(0, 144), (144, 512)],
         chunks=[("sample", 0, 128), ("lead", 128, 16)] + [("full", 144 + 128 * i, 128) for i in range(4)],
         pstart=128, plen=528, p0=None),
    dict(T=768, ngs=[(0, 384), (384, 384)], chunks=[("full", 128 * i, 128) for i in range(6)],
         pstart=0, plen=768, p0=512),
    dict(T=768, ngs=[(0, 384), (384, 384)], chunks=[("full", 128 * i, 128) for i in range(6)],
         pstart=0, plen=768, p0=1280),
]
TMAX = 768
NCHMAX = 6

C_Q, C_K, C_V, C_G, C_BG, C_CG, C_HC, C_GA, C_GB = [i * 1024 for i in range(9)]


def weight_groups():
    g = []
    for kind in ("wk", "wq", "wv", "wg"):
        g.append((kind, 0, 4096))
    for kind in ("wv", "wq", "wk", "wg"):
        g.append((kind, 1, 4096))
    for c in range(8):
        g.append(("conv", c, 3072))
    for mp in range(4):
        g.append(("garo", mp, 4096))
    for mp in range(4):
        g.append(("gbco", mp, 4096))
    for i in range(2):
        g.append(("wo", i, 4096))
    for fp in range(11):
        g.append(("ffn1", fp, 4096))
    for m in range(8):
        g.append(("wd", m, 2816))
    return g


WG = weight_groups()
WUNIQ = []
for _g in WG:
    if _g not in WUNIQ:
        WUNIQ.append(_g)
_uoff = np.cumsum([0] + [x[2] for x in WUNIQ]).tolist()
WTOT = _uoff[-1]
WOFF = [_uoff[WUNIQ.index(_g)] for _g in WG]
NSLOT = 3
import os
DBG = os.environ.get('KDBG') == '1'
POLICY = os.environ.get('KPOL', 'fixed')
MINS = int(os.environ.get('KMINS', '1'))
MARGIN = float(os.environ.get('KMARGIN', '0.5'))
SLO = float(os.environ.get('KSLO', '0'))
HOP = float(os.environ.get('KHOP', '0.8'))
KDVE = float(os.environ.get('KDVE', '1'))
KACT = float(os.environ.get('KACT', '1'))
KPOOL = float(os.environ.get('KPOOL', '1'))
SIDE_RATIO = int(os.environ.get('KSR', '2'))
MERGE01 = os.environ.get('KM01', '0') == '1'
MERGE3 = os.environ.get('KM3', '0') == '1'
POOL_ENG = os.environ.get('KPOOLENG', 'pool')
GSIDE = os.environ.get('KGSIDE', '1') == '1'
NSET_S = int(os.environ.get('KNSETS', '4'))
SG0_EARLY = os.environ.get('KSG0E', '0') == '1'
SPLIT_SAMPLE = os.environ.get('KSPLIT', '0') == '1'
RMS_LAG = int(os.environ.get('KRMSLAG', '2'))
SHI = float(os.environ.get('KSHI', '0'))
SLOTSZ = 4096


def _unit(W, c0):
    return W[:, c0:c0 + 128].reshape(8, 128, 128).transpose(1, 0, 2).reshape(128, 1024)


def pack_weights(w_in, w_ret_o, w_conv_o, w_o, w_gate_up, w_down):
    out = np.empty((DEPTH, 128, WTOT), np.float32)
    for l in range(DEPTH):
        col = 0
        for kind, idx, sz in WUNIQ:
            if kind == "wv":
                blk = w_in[l][:, C_V + idx * 512:C_V + (idx + 1) * 512].reshape(8, 128, 512).transpose(1, 0, 2).reshape(128, 4096)
            elif kind in ("wq", "wk", "wg"):
                base = {"wq": C_Q, "wk": C_K, "wg": C_G}[kind]
                blk = np.concatenate([_unit(w_in[l], base + (idx * 4 + j) * 128) for j in range(4)], axis=1)
            elif kind == "garo":
                blk = np.concatenate([_unit(w_in[l], C_GA + (2 * idx) * 128), _unit(w_ret_o[l], (2 * idx) * 128),
                                      _unit(w_in[l], C_GA + (2 * idx + 1) * 128), _unit(w_ret_o[l], (2 * idx + 1) * 128)], axis=1)
            elif kind == "conv":
                blk = np.concatenate([_unit(w_in[l], C_CG + idx * 128), _unit(w_in[l], C_HC + idx * 128),
                                      _unit(w_in[l], C_BG + idx * 128)], axis=1)
            elif kind == "gbco":
                blk = np.concatenate([_unit(w_in[l], C_GB + (2 * idx) * 128), _unit(w_conv_o[l], (2 * idx) * 128),
                                      _unit(w_in[l], C_GB + (2 * idx + 1) * 128), _unit(w_conv_o[l], (2 * idx + 1) * 128)], axis=1)
            elif kind == "wo":
                blk = np.concatenate([_unit(w_o[l], (4 * idx + j) * 128) for j in range(4)], axis=1)
            elif kind == "ffn1":
                blk = np.concatenate([_unit(w_gate_up[l], (2 * idx) * 128), _unit(w_gate_up[l], DFF + (2 * idx) * 128),
                                      _unit(w_gate_up[l], (2 * idx + 1) * 128), _unit(w_gate_up[l], DFF + (2 * idx + 1) * 128)], axis=1)
            elif kind == "wd":
                blk = w_down[l][:, idx * 128:(idx + 1) * 128].reshape(NF, 128, 128).transpose(1, 0, 2).reshape(128, NF * 128)
            out[l][:, col:col + sz] = blk
            col += sz
    return out


class Src:
    def __init__(self, name, mult):
        self.name, self.mult, self.count, self.sem = name, mult, 0, None


class Buf:
    __slots__ = ("name", "w", "r")

    def __init__(self, name):
        self.name, self.w, self.r = name, None, {}


class Prog:
    ENG = ("pe", "act", "dve", "pool", "sp")

    def __init__(self):
        self.src = {e: Src(e, 1) for e in self.ENG}
        self.ops = {e: [] for e in self.ENG}
        self.seen = {e: {} for e in self.ENG}
        self.chans = []
        self.t = {e: 0.0 for e in self.ENG}
        self.fin = {}

    def chan(self, name):
        c = Src(name, 16)
        self.chans.append(c)
        return c

    def _deps(self, eng, reads, writes):
        deps = {}

        def add(s, c):
            if deps.get(s, 0) < c:
                deps[s] = c
        for b in reads:
            if b.w is not None:
                add(*b.w)
        for b in writes:
            if b.w is not None:
                add(*b.w)
            for s, c in b.r.items():
                add(s, c)
        me = self.src[eng]
        seen = self.seen[eng]
        out = []
        self._ready = max([self.fin.get((s, c), 0.0) for s, c in deps.items() if s is not me] + [0.0])
        self._alldeps = deps
        for s, c in deps.items():
            if s is me:
                continue
            if seen.get(s, 0) >= c:
                continue
            seen[s] = c
            out.append((s, c))
        if eng in ("act", "dve", "pool"):
            c = 0
            for b in reads:
                if b.w is not None and b.w[0] is me:
                    c = max(c, b.w[1])
            if c > seen.get(me, 0):
                seen[me] = c
                out.append((me, c))
        return out

    def op(self, eng, fn, reads=(), writes=(), signal=True, cost=0.3):
        waits = self._deps(eng, reads, writes)
        me = self.src[eng]
        cnt = me.count + 1
        if signal:
            me.count = cnt
        start = max(self.t[eng], self._ready + HOP)
        if DBG and eng == "pe" and start - self.t[eng] > 1.0 and SLO < start < SHI:
            blk = max(((self.fin.get((s_, c_), 0.0), s_.name) for s_, c_ in self._alldeps.items() if s_ is not me), default=None)
            print(f"[sim] PE stall {self.t[eng]:.1f} -> {start:.1f} ({start - self.t[eng]:.1f}us) waiting on {blk} writes={[b.name for b in writes]} reads={[b.name for b in reads][:3]}")
        self.t[eng] = start + cost
        self.fin[(me, cnt)] = self.t[eng]
        self.busy = getattr(self, "busy", {})
        self.busy[eng] = self.busy.get(eng, 0.0) + cost
        for b in reads:
            b.r[me] = cnt
        for b in writes:
            b.w = (me, cnt)
            b.r = {}
        self.ops[eng].append((waits, fn, me if signal else None, 1))

    def dma(self, queue, fn, chan, reads=(), writes=(), cost=4.0):
        waits = self._deps(queue, reads, writes)
        chan.count += 1
        start = max(self.t[queue], self._ready + HOP)
        self.t[queue] = start + 0.1
        self.fin[(chan, chan.count)] = start + cost
        for b in reads:
            b.r[chan] = chan.count
        for b in writes:
            b.w = (chan, chan.count)
            b.r = {}
        self.ops[queue].append((waits, fn, chan, 16))

    def barrier(self, engines=("pe", "act", "dve", "sp", "pool")):
        for e in engines:
            waits = []
            for s in list(self.src.values()) + self.chans:
                if s is self.src[e] or s.count == 0:
                    continue
                if self.seen[e].get(s, 0) >= s.count:
                    continue
                self.seen[e][s] = s.count
                waits.append((s, s.count))
            if waits:
                self.ops[e].append((waits, None, None, 0))

    def emit(self, block):
        decos = {"pe": block.tensor, "act": block.scalar, "dve": block.vector, "pool": block.gpsimd, "sp": block.sync}
        for eng in self.ENG:
            ops = self.ops[eng]

            def body(e, ops=ops):
                for waits, fn, sig, inc in ops:
                    for s, c in waits:
                        e.wait_ge(s.sem, c * s.mult)
                    if fn is None:
                        continue
                    ins = fn(e)
                    if sig is not None:
                        ins.then_inc(sig.sem, inc)
            decos[eng](body)


def build_program():
    nc = bass.Bass("TRN2", target_bir_lowering=False)
    P = Prog()
    es = ExitStack()

    def din(name, shape):
        return nc.dram_tensor(name, list(shape), F32, kind="ExternalInput").ap()

    def dout(name, shape):
        return nc.dram_tensor(name, list(shape), F32, kind="ExternalOutput").ap()

    x_in = [din(f"x{i}", (128, NK, SGS[i]["T"])) for i in range(3)]
    cs_in = [din(f"cs{i}", (128, 2, SGS[i]["T"])) for i in range(3)]
    wpk = din("wpk", (DEPTH, 128, WTOT))
    dmask_in = din("dmask", (128, HEADS, 128))
    smask_in = din("smask", (128, HEADS, 128))
    qdec_in = din("qdec", (128, HEADS, 128))
    qdecs_in = din("qdecs", (128, HEADS, 128))
    kdec_in = din("kdec", (128, 3, HEADS))
    rm3_in = din("rm3", (128, NSS, 128))
    ident_in = din("ident", (128, 128))
    gains_in = din("gains", (128, 5, NK))
    convw_in = din("convw", (128, DEPTH, NK, 3))
    sret_in = din("sret", (DEPTH, 128, HEADS, NSS, 128))
    sconv_in = din("sconv", (128, DEPTH, NK, NSS, 2))
    y_out = [dout(f"y{i}", (128, NK, SGS[i]["T"])) for i in range(3)]
    osp = dout("osp", (DEPTH, 128, 1024))
    ocp = dout("ocp", (128, DEPTH, NK, 2))
    oss = dout("oss", (DEPTH, 128, HEADS, NSS, 128))
    ocs = dout("ocs", (128, DEPTH, NK, NSS, 2))

    uniq = [0]

    def sb(name, shape, dt=F32, stack=None):
        uniq[0] += 1
        return (stack or es).enter_context(nc.sbuf_tensor(f"{name}_{uniq[0]}", list(shape), dt))

    def ps(name, shape, dt=F32):
        return es.enter_context(nc.psum_tensor(name, list(shape), dt))

    H = sb("H", (128, NK, TMAX))
    A = sb("A", (128, NK, TMAX), BF16)
    O = sb("O", (128, NK, TMAX), BF16)
    SST = sb("SST", (128, DEPTH, 1024))
    SBF = sb("SBF", (128, DEPTH, 1024), BF16)
    CARRY = sb("CARRY", (128, DEPTH, NK, 2))
    CSS = sb("CSS", (128, DEPTH, NK, NSS, 2))
    CPV = sb("CPV", (128, DEPTH, NK, NSS, 2))
    WSL = [sb(f"WSL{i}", (128, SLOTSZ), BF16) for i in range(NSLOT)]
    CS = sb("CS", (128, 2, TMAX))
    DMASK = sb("DMASK", (128, HEADS, 128))
    QDEC = sb("QDEC", (128, HEADS, 128))
    KDEC = sb("KDEC", (128, 3, HEADS))
    IDENT = sb("IDENT", (128, 128), BF16)
    ONES = sb("ONES", (128, 128), BF16)
    GAINS = sb("GAINS", (128, 5, NK))
    CONVW = sb("CONVW", (128, DEPTH, NK, 3))
    EPSC = sb("EPSC", (128, 1))

    PS = [ps(f"PS{i}", (128, 512)) for i in range(8)]
    bPS = [Buf(f"PS{i}") for i in range(8)]

    bH = [[Buf(f"H{k}_{g}") for g in range(2)] for k in range(NK)]
    bA = [[Buf(f"A{k}_{g}") for g in range(2)] for k in range(NK)]

    class AK:
        def __init__(self, g):
            self.g = g
    bO = [[Buf(f"O{h}_{g}") for g in range(2)] for h in range(HEADS)]
    bSST = [[Buf(f"SST{l}_{hh}") for hh in range(2)] for l in range(DEPTH)]
    bSBF = [[Buf(f"SBF{l}_{hh}") for hh in range(2)] for l in range(DEPTH)]
    bCARRY = [[Buf(f"CARRY{l}_{c}") for c in range(NK)] for l in range(DEPTH)]
    bCSS = [[Buf(f"CSS{l}_{c}") for c in range(NK)] for l in range(DEPTH)]
    bWSL = [Buf(f"WSL{i}") for i in range(NSLOT)]
    bCS = Buf("CS")
    bCONST = Buf("CONST")

    ch_w = [P.chan(f"w{i}") for i in range(NSLOT)]
    ch_const = P.chan("const")
    ch_constsw = P.chan("constsw")
    ch_sc = P.chan("sc")
    ch_scsw = P.chan("scsw")
    ch_x = P.chan("x")
    ch_cs = P.chan("cs")
    ch_y = P.chan("y")
    ch_small = P.chan("small")
    ch_sin = [P.chan(f"sin{i}") for i in range(4)]
    ch_sinbf = [P.chan(f"sinbf{i}") for i in range(2)]
    ch_sout = [P.chan(f"sout{i}") for i in range(4)]

    def fsz(ap):
        n_ = 1
        for d_ in ap.shape[1:]:
            n_ *= d_
        return n_

    def ecost(eng, ap):
        f_ = fsz(ap)
        if eng == "dve":
            return (f_ / 960.0 + 0.07) * KDVE
        if eng == "act":
            return (f_ / 1200.0 + 0.22) * KACT
        return (f_ * 0.0022 + 0.15) * KPOOL

    def mm(out, lhsT, rhs, start, stop, reads, writes, signal):
        P.op("pe", lambda e: e.matmul(out, lhsT=lhsT, rhs=rhs, start=start, stop=stop), reads, writes, signal,
             cost=max(fsz(out), 64) / 2400.0 + 0.008)

    def tr(out, in_, ident, reads, writes, signal=True):
        P.op("pe", lambda e: e.transpose(out, in_, ident), list(reads) + [bIDENT], writes, signal, cost=0.07)

    def tt(eng, out, in0, in1, op, reads, writes):
        P.op(eng, lambda e: e.tensor_tensor(out=out, in0=in0, in1=in1, op=op), reads, writes, cost=ecost(eng, out))

    def ts(eng, out, in0, s1, s2, op0, op1, reads, writes):
        if s2 is None:
            P.op(eng, lambda e: e.tensor_scalar(out=out, in0=in0, scalar1=s1, scalar2=None, op0=op0), reads, writes, cost=ecost(eng, out))
        else:
            P.op(eng, lambda e: e.tensor_scalar(out=out, in0=in0, scalar1=s1, scalar2=s2, op0=op0, op1=op1), reads, writes, cost=ecost(eng, out))

    def stt(eng, out, in0, scalar, in1, op0, op1, reads, writes):
        P.op(eng, lambda e: e.scalar_tensor_tensor(out=out, in0=in0, scalar=scalar, in1=in1, op0=op0, op1=op1), reads, writes, cost=ecost(eng, out))

    def act(out, in_, func, reads, writes, bias=None, scale=None):
        kw = {}
        if bias is not None:
            kw["bias"] = bias
        if scale is not None:
            kw["scale"] = scale
        P.op("act", lambda e: e.activation(out=out, in_=in_, func=func, **kw), reads, writes, cost=ecost("act", out))

    def cp(eng, out, in_, reads, writes):
        if eng == "act":
            P.op("act", lambda e: e.copy(out=out, in_=in_), reads, writes, cost=ecost("act", out))
        else:
            P.op(eng, lambda e: e.tensor_copy(out=out, in_=in_), reads, writes, cost=ecost(eng, out))

    def dma(queue, out, in_, chan, reads, writes):
        nbytes = 128 * fsz(in_) * 4
        P.dma(queue, lambda e: e.dma_start(out=out, in_=in_), chan, reads, writes, cost=2.0 + nbytes / 340e3)

    for dst, src in ((DMASK, dmask_in), (QDEC, qdec_in), (KDEC, kdec_in), (GAINS, gains_in), (CONVW, convw_in), (CPV, sconv_in)):
        dma("sp", dst[:], src[:], ch_const, [], [])
    bCONST.w = (ch_const, ch_const.count)
    P.fin[(ch_const, ch_const.count)] = max(P.fin.get((ch_const, c_), 0.0) for c_ in range(1, ch_const.count + 1))
    bIDENT = Buf("IDENT")
    dma("pool", IDENT[:], ident_in[:], ch_constsw, [], [bIDENT])
    P.op("dve", lambda e: e.memset(ONES[:], 1.0), [], [bCONST])
    P.op("dve", lambda e: e.memset(EPSC[:], EPS), [], [bCONST])
    P.op("dve", lambda e: e.memset(CSS[:], 0.0), [], [bCONST])

    def wg_order(sgi_):
        idx = list(range(len(WG)))
        if SG0_EARLY and sgi_ == 0:
            want = [("wk", 0), ("wq", 0), ("wk", 1), ("wq", 1), ("wv", 0), ("wv", 1), ("wg", 0), ("wg", 1)]
            head = [next(i for i in idx if WG[i][0] == k_ and WG[i][1] == h_) for k_, h_ in want]
            idx = head + [i for i in idx if i not in head]
        return idx
    wseq = [(l, gi) for sg_ in range(3) for l in range(DEPTH) for gi in wg_order(sg_)]
    wstate = {"issued": 0, "cur": -1}

    def w_issue_upto(n):
        while wstate["issued"] < min(n, len(wseq)):
            i = wstate["issued"]
            l, gi = wseq[i]
            sz = WG[gi][2]
            s = i % NSLOT
            dma("pool", WSL[s][:, 0:sz], wpk[l][:, WOFF[gi]:WOFF[gi] + sz], ch_w[s], [], [bWSL[s]])
            wstate["issued"] += 1

    def w_next(expect_kind):
        wstate["cur"] += 1
        i = wstate["cur"]
        assert WG[wseq[i][1]][0] == expect_kind, (WG[wseq[i][1]], expect_kind)
        w_issue_upto(i + NSLOT)
        s = i % NSLOT
        return WSL[s], bWSL[s]

    w_issue_upto(NSLOT)

    class AccPool:
        def __init__(self, idxs):
            self.idxs, self.i = idxs, 0

        def next(self):
            k = self.idxs[self.i % len(self.idxs)]
            self.i += 1
            return PS[k], bPS[k]

    def ng_of(sg, t0):
        for gi, (a, n) in enumerate(sg["ngs"]):
            if a <= t0 < a + n:
                return gi
        raise AssertionError

    def dense_acc(acc, bacc, wsl, bw, woff, nkc, X, xk_stride_fn, a, n, bx):
        for kc in range(nkc):
            bxk = [bA[kc][bx.g]] if isinstance(bx, AK) else bx
            mm(acc[:, 0:n], wsl[:, woff + kc * 128: woff + (kc + 1) * 128], xk_stride_fn(kc, a, n),
               kc == 0, kc == nkc - 1, [bw] + bxk, [bacc], kc == nkc - 1)

    SQK = [sb(f"SQK{i}", (128, 512), BF16) for i in range(2)]
    RSP = [sb(f"RSP{i}", (128, 512), F32) for i in range(2)]
    bSQK = [Buf(f"SQK{i}") for i in range(2)]
    bRSP = [Buf(f"RSP{i}") for i in range(2)]
    rms_cnt = [0, 0]

    def rmsnorm(sg, gi_gain, out_fn, bout_fn, stack=None):
        pool = AccPool([0, 1])
        for g, (a, n) in enumerate(sg["ngs"]):
            i = rms_cnt[0] % 2
            rms_cnt[0] += 1
            acc, bacc = pool.next()
            for kc in range(NK):
                q = rms_cnt[1] % 2
                rms_cnt[1] += 1
                act(SQK[q][:, 0:n], H[:, kc, a:a + n], AF.Square, [bH[kc][g]], [bSQK[q]])
                mm(acc[:, 0:n], ONES[:, :], SQK[q][:, 0:n], kc == 0, kc == NK - 1, [bSQK[q], bCONST], [bacc], True)
            act(RSP[i][:, 0:n], acc[:, 0:n], AF.Sqrt, [bCONST], [bacc, bRSP[i]], bias=EPSC[:, 0:1], scale=1.0 / D)
            P.op("dve", lambda e, o=RSP[i][:, 0:n]: e.reciprocal(out=o, in_=o), [bRSP[i]], [bRSP[i]])
            for kc in range(NK):
                stt("dve", out_fn(kc, a, n), H[:, kc, a:a + n], GAINS[:, gi_gain, kc:kc + 1], RSP[i][:, 0:n],
                    ALU.mult, ALU.mult, [bH[kc][g], bRSP[i], bCONST], [bout_fn(kc, g)])

    RMSB = [6, 7]

    def rms_accum(sg, g, kc):
        a, n = sg["ngs"][g]
        q = rms_cnt[1] % 2
        rms_cnt[1] += 1
        acc, bacc = PS[RMSB[g]], bPS[RMSB[g]]
        act(SQK[q][:, 0:n], H[:, kc, a:a + n], AF.Square, [bH[kc][g]], [bSQK[q]])
        mm(acc[:, 0:n], ONES[:, :], SQK[q][:, 0:n], kc == 0, kc == NK - 1, [bSQK[q], bCONST], [bacc], True)

    rms_q = []

    def rms_accum_lag(sg, g, kc, lag=RMS_LAG):
        rms_q.append((sg, g, kc))
        while len(rms_q) > lag:
            rms_accum(*rms_q.pop(0))

    def rms_finish(sg, gi_gain, out_fn, bout_fn):
        while rms_q:
            rms_accum(*rms_q.pop(0))
        for g, (a, n) in enumerate(sg["ngs"]):
            i = rms_cnt[0] % 2
            rms_cnt[0] += 1
            acc, bacc = PS[RMSB[g]], bPS[RMSB[g]]
            act(RSP[i][:, 0:n], acc[:, 0:n], AF.Sqrt, [bCONST], [bacc, bRSP[i]], bias=EPSC[:, 0:1], scale=1.0 / D)
            P.op("dve", lambda e, o=RSP[i][:, 0:n]: e.reciprocal(out=o, in_=o), [bRSP[i]], [bRSP[i]])
            for kc in range(NK):
                stt("dve", out_fn(kc, a, n), H[:, kc, a:a + n], GAINS[:, gi_gain, kc:kc + 1], RSP[i][:, 0:n],
                    ALU.mult, ALU.mult, [bH[kc][g], bRSP[i], bCONST], [bout_fn(kc, g)])

    def interleave(task_fns, sets, side=None, side_ratio=1):
        active = []
        free = list(sets)
        it = iter(task_fns)
        pending = next(it, None)
        side_alive = side is not None
        while True:
            if pending is not None and free:
                fn_, ready_ = pending if isinstance(pending, tuple) else (pending, None)
                if ready_ is None or ready_():
                    cs_ = free.pop(0)
                    active.append((fn_(cs_), cs_))
                    pending = next(it, None)
                    if DBG and SLO < P.t['pe'] < SHI: print(f"[sim]   admit task at pe={P.t['pe']:.0f} dve={P.t['dve']:.0f} nactive={len(active)} side_alive={side_alive}")
            if not active and pending is None and not side_alive:
                break
            for ent in list(active):
                if next(ent[0], "END") == "END":
                    active.remove(ent)
                    free.append(ent[1])
                    if DBG and SLO < P.t['pe'] < SHI: print(f"[sim]   task done at pe={P.t['pe']:.0f} dve={P.t['dve']:.0f} act={P.t['act']:.0f}")
            if side_alive:
                nst = 0
                while True:
                    busy_others = max(P.t["dve"], P.t["act"])
                    if POLICY == "fixed":
                        sr_ = side_ratio() if callable(side_ratio) else side_ratio
                        if (active or pending is None) and nst >= sr_ and active:
                            break
                        if not active and pending is not None and nst >= 1:
                            break
                    else:
                        if active and nst >= MINS and P.t["pe"] >= busy_others - MARGIN:
                            break
                        if active and nst >= 6:
                            break
                    if next(side, "END") == "END":
                        side_alive = False
                        break
                    nst += 1

    def run(gen):
        for _ in gen:
            pass

    def load_sg(i_):
        T_ = SGS[i_]["T"]
        for kc in range(NK):
            dma("sp", H[:, kc, 0:T_], x_in[i_][:, kc, :], ch_x, [], [bH[kc][g] for g in range(2)])
        for kc in range(NK):
            for g in range(2):
                bH[kc][g].w = (ch_x, ch_x.count)
                P.fin[(ch_x, ch_x.count)] = max(P.fin.get((ch_x, c_), 0.0) for c_ in range(ch_x.count - NK + 1, ch_x.count + 1))
        for j in range(2):
            dma("sp", CS[:, j, 0:T_], cs_in[i_][:, j, :], ch_cs, [], [bCS])

    for sgi, sg in enumerate(SGS):
        T = sg["T"]
        ngs = sg["ngs"]
        chunks = sg["chunks"]
        nch = len(chunks)
        has_sample = any(k == "sample" for k, _, _ in chunks)
        overlap_conv = not has_sample
        if sgi == 0:
            load_sg(0)
        sg_stack = ExitStack()
        if has_sample:
            SMASK = sb("SMASK", (128, HEADS, 128), F32, sg_stack)
            QDECS = sb("QDECS", (128, HEADS, 128), F32, sg_stack)
            RM3 = sb("RM3", (128, NSS, 128), BF16, sg_stack)
            bSC = Buf("SC")
            dma("sp", SMASK[:], smask_in[:], ch_sc, [], [bSC])
            dma("sp", QDECS[:], qdecs_in[:], ch_sc, [], [bSC])
            dma("pool", RM3[:], rm3_in[:], ch_scsw, [], [bSC])

        for l in range(DEPTH):
            if l == 0:
                if sgi == 0:
                    rmsnorm(sg, l, lambda kc, a, n: A[:, kc, a:a + n], lambda kc, g: bA[kc][g])
            else:
                rms_finish(sg, l, lambda kc, a, n: A[:, kc, a:a + n], lambda kc, g: bA[kc][g])

            conv_stack = ExitStack()

            def conv_alloc():
                cx = {}
                cx["YB"] = sb("YB", (128, NK, T), BF16, conv_stack)
                cx["bYB"] = [Buf(f"YB{g}") for g in range(2)]
                cx["UP"] = [sb(f"UP{i}", (128, 2 + T), F32, conv_stack) for i in range(2)]
                cx["US"] = [sb(f"US{i}", (128, NSS, DS + 2), F32, conv_stack) for i in range(2)] if sgi == 0 else [None, None]
                cx["BG"] = [sb(f"BG{i}", (128, T), F32, conv_stack) for i in range(2)]
                cx["YC"] = [sb(f"YC{i}", (128, T), F32, conv_stack) for i in range(2)]
                cx["CGS"] = [sb(f"CGS{i}", (128, max(n_ for _, n_ in ngs)), F32, conv_stack) for i in range(2)]
                for nm in ("UP", "US", "BG", "YC", "CGS"):
                    cx["b" + nm] = [Buf(f"{nm}{i}") for i in range(2)]
                return cx

            def conv_gen(cx, pool):
                YB, bYB = cx["YB"], cx["bYB"]
                UP, US, BG, YC, CGS = cx["UP"], cx["US"], cx["BG"], cx["YC"], cx["CGS"]
                bUP, bUS, bBG, bYC, bCGS = cx["bUP"], cx["bUS"], cx["bBG"], cx["bYC"], cx["bCGS"]
                plen, pstart = sg["plen"], sg["pstart"]
                it = 0
                for c in range(NK):
                    w, bw = w_next("conv")
                    u = c % 2
                    if sgi == 0:
                        P.op("dve", lambda e, u=u: e.memset(UP[u][:, 0:2], 0.0), [], [bUP[u]])
                        cp("dve", US[u][:, :, 0:2], CPV[:, l, c, :, :], [bCONST], [bUS[u]])
                    else:
                        cp("dve", UP[u][:, 0:2], CARRY[:, l, c, :], [bCARRY[l][c]], [bUP[u]])
                    for g, (a, n) in enumerate(ngs):
                        i = it % 2
                        it += 1
                        acc1, bacc1 = pool.next()
                        acc2, bacc2 = pool.next()
                        acc3, bacc3 = pool.next()
                        xa = lambda kc, a, n: A[:, kc, a:a + n]
                        dense_acc(acc1, bacc1, w, bw, 0, NK, A, xa, a, n, AK(g))
                        dense_acc(acc2, bacc2, w, bw, 1024, NK, A, xa, a, n, AK(g))
                        dense_acc(acc3, bacc3, w, bw, 2048, NK, A, xa, a, n, AK(g))
                        cp("act", CGS[i][:, 0:n], acc1[:, 0:n], [], [bacc1, bCGS[i]])
                        if sgi == 0 and g == 0:
                            tt("dve", US[u][:, :, 2:2 + DS], acc2[:, 0:128].rearrange("p (s i) -> p s i", s=NSS),
                               CGS[i][:, 0:128].rearrange("p (s i) -> p s i", s=NSS), ALU.mult, [bCGS[i]], [bacc2, bUS[u]])
                            tt("dve", UP[u][:, 2:2 + 16], acc2[:, 128:144], CGS[i][:, 128:144], ALU.mult, [bCGS[i]], [bacc2, bUP[u]])
                        else:
                            o0 = 2 + (a - pstart)
                            tt("dve", UP[u][:, o0:o0 + n], acc2[:, 0:n], CGS[i][:, 0:n], ALU.mult, [bCGS[i]], [bacc2, bUP[u]])
                        cp("act", BG[u][:, a:a + n], acc3[:, 0:n], [], [bacc3, bBG[u]])
                        yield
                    yp = YC[u][:, pstart:pstart + plen]
                    act(yp, UP[u][:, 0:plen], AF.Identity, [bUP[u], bCONST], [bYC[u]], scale=CONVW[:, l, c, 0:1])
                    stt("dve", yp, UP[u][:, 1:1 + plen], CONVW[:, l, c, 1:2], yp, ALU.mult, ALU.add, [bUP[u], bYC[u], bCONST], [bYC[u]])
                    stt("dve", yp, UP[u][:, 2:2 + plen], CONVW[:, l, c, 2:3], yp, ALU.mult, ALU.add, [bUP[u], bYC[u], bCONST], [bYC[u]])
                    if sgi == 0:
                        ys = YC[u][:, 0:128].rearrange("p (s i) -> p s i", s=NSS)
                        ts("dve", ys, US[u][:, :, 0:DS], CONVW[:, l, c, 0:1], None, ALU.mult, None, [bUS[u], bCONST], [bYC[u]])
                        stt("dve", ys, US[u][:, :, 1:1 + DS], CONVW[:, l, c, 1:2], ys, ALU.mult, ALU.add, [bUS[u], bYC[u], bCONST], [bYC[u]])
                        stt("dve", ys, US[u][:, :, 2:2 + DS], CONVW[:, l, c, 2:3], ys, ALU.mult, ALU.add, [bUS[u], bYC[u], bCONST], [bYC[u]])
                        cp("dve", CSS[:, l, c, :, :], US[u][:, :, DS:DS + 2], [bUS[u]], [bCSS[l][c]])
                    cp("dve", CARRY[:, l, c, :], UP[u][:, plen:plen + 2], [bUP[u]], [bCARRY[l][c]])
                    yield
                    tt(POOL_ENG, YB[:, c, 0:T], BG[u][:, 0:T], YC[u][:, 0:T], ALU.mult, [bBG[u], bYC[u]], [bYB[0], bYB[1]])
                    yield

            cx = conv_alloc() if overlap_conv else None

            with ExitStack() as st:
                QT = [sb(f"QT{hh}", (128, 4, T), BF16, st) for hh in range(2)]
                KT = [sb(f"KT{hh}", (128, 4, T), BF16, st) for hh in range(2)]
                V = [sb(f"V{hh}", (128, nch, 512), BF16, st) for hh in range(2)]
                RA = [sb(f"RA{i}", (128, 512), F32, st) for i in range(2)]
                RB = [sb(f"RB{i}", (128, 512), F32, st) for i in range(2)]
                bQT = [[Buf(f"QT{hh}_{g}") for g in range(2)] for hh in range(2)]
                bKT = [[Buf(f"KT{hh}_{g}") for g in range(2)] for hh in range(2)]
                bV = [[Buf(f"V{hh}_{c}") for c in range(nch)] for hh in range(2)]
                bRA = [Buf(f"RA{i}") for i in range(2)]
                bRB = [Buf(f"RB{i}") for i in range(2)]
                NSET = NSET_S if has_sample else 4
                set_banks = [4, 5, 6, 7, 3, 2][:NSET]
                csets = []
                for si_ in range(NSET):
                    d_ = dict(ps=PS[set_banks[si_]], bps=bPS[set_banks[si_]])
                    for nm, shp, dt_ in (("PTm", (128, 4, 128), BF16), ("QTL", (128, 4, 128), BF16),
                                         ("ON", (128, 4, 128), BF16), ("KH", (128, 4, 128), BF16), ("STAT", (128, 8, 4), F32)):
                        d_[nm] = sb(f"{nm}s{si_}", shp, dt_, st)
                        d_["b" + nm] = Buf(f"{nm}s{si_}")
                    csets.append(d_)
                SQOs = [sb(f"SQO{i}", (128, 4, 128), F32, st) for i in range(2)]
                bSQOs = [Buf(f"SQO{i}") for i in range(2)]
                sqo_i = [0]
                if has_sample:
                    QM = sb("QM", (128, NSS, 128), BF16, st)
                    KM = sb("KM", (128, NSS, 128), BF16, st)
                    SINQ = [sb(f"SINQ{i}", (128, 4, 128), F32, st) for i in range(4)]
                    SINBFH = [sb(f"SINBFH{i}", (128, 8, 128), BF16, st) for i in range(2)]
                    bSINQ = [Buf(f"SINQ{i}") for i in range(4)]
                    bSINBFH = [Buf(f"SINBFH{i}") for i in range(2)]
                    bQM, bKM = Buf("QM"), Buf("KM")
                    P.op("dve", lambda e: e.memset(QM[:], 0.0), [], [bQM])
                    P.barrier(engines=("sp",))
                rot_i = [0]

                def rotary(acc, bacc, dst, bdst, a, n):
                    i = rot_i[0] % 2
                    rot_i[0] += 1
                    tt("dve", RA[i][:, 0:n], acc[:, 0:n], CS[:, 0, a:a + n], ALU.mult, [bCS], [bacc, bRA[i]])
                    tt("dve", RB[i][0:64, 0:n], acc[64:128, 0:n], CS[64:128, 1, a:a + n], ALU.mult, [bCS], [bacc, bRB[i]])
                    tt("dve", RB[i][64:128, 0:n], acc[0:64, 0:n], CS[0:64, 1, a:a + n], ALU.mult, [bCS], [bacc, bRB[i]])
                    tt(POOL_ENG, dst, RA[i][:, 0:n], RB[i][:, 0:n], ALU.add, [bRA[i], bRB[i]], [bdst])

                dpool = AccPool([b_ for b_ in range(4) if b_ not in set_banks])

                def dense_gen(hh, parts=("v", "q", "k", "g")):
                    h0 = hh * 4
                    pool = dpool
                    xa = lambda kc, a, n: A[:, kc, a:a + n]
                    for part in parts:
                        if part == "v":
                            wv, bwv = w_next("wv")
                            for ci, (kind, t0, L) in enumerate(chunks):
                                g = ng_of(sg, t0)
                                acc, bacc = pool.next()
                                for kc in range(NK):
                                    mm(acc[0:L, 0:512], A[:, kc, t0:t0 + L], wv[:, kc * 512:(kc + 1) * 512], kc == 0, kc == NK - 1,
                                       [bwv, bA[kc][g]], [bacc], kc == NK - 1)
                                cp("act", V[hh][0:L, ci, :], acc[0:L, 0:512], [], [bacc, bV[hh][ci]])
                                vdone[hh] = ci + 1
                                yield
                        elif part in ("q", "k"):
                            w_, bw_ = w_next("w" + part)
                            dstT, bdstT = (QT, bQT) if part == "q" else (KT, bKT)
                            for j in range(4):
                                for g, (a, n) in enumerate(ngs):
                                    acc, bacc = pool.next()
                                    dense_acc(acc, bacc, w_, bw_, j * 1024, NK, A, xa, a, n, AK(g))
                                    rotary(acc, bacc, dstT[hh][:, j, a:a + n], bdstT[hh][g], a, n)
                                    yield
                        else:
                            wg_, bwg = w_next("wg")
                            for j in range(4):
                                for g, (a, n) in enumerate(ngs):
                                    acc, bacc = pool.next()
                                    dense_acc(acc, bacc, wg_, bwg, j * 1024, NK, A, xa, a, n, AK(g))
                                    act(O[:, h0 + j, a:a + n], acc[:, 0:n], AF.Silu, [], [bacc, bO[h0 + j][g]])
                                    yield

                def chunk_gen(hh, ci, cs_, part="all"):
                    kind, t0, L = chunks[ci]
                    h0 = hh * 4
                    g = ng_of(sg, t0)
                    ps_, bps = cs_["ps"], cs_["bps"]
                    bpt = bps
                    PTm, QTL, ON, KH, STAT = (cs_[k_] for k_ in ("PTm", "QTL", "ON", "KH", "STAT"))
                    bPTm, bQTL, bON, bKH, bSTAT = (cs_["b" + k_] for k_ in ("PTm", "QTL", "ON", "KH", "STAT"))
                    SQO, bSQO = SQOs[sqo_i[0] % 2], bSQOs[sqo_i[0] % 2]
                    sqo_i[0] += 1
                    ps3 = ps_[:].rearrange("p (j i) -> p j i", j=4)
                    ps_bf = ps_[:, 0:256].bitcast(BF16)
                    ps_k = ps_bf
                    ps_t = ps_bf
                    ps_k3 = ps_k.rearrange("p (j i) -> p j i", j=4)
                    ps_t3 = ps_t.rearrange("p (j i) -> p j i", j=4)
                    qt, kt, v = QT[hh], KT[hh], V[hh]
                    bqt, bkt, bv = bQT[hh][g], bKT[hh][g], bV[hh][ci]
                    kd = {"full": 0, "lead": 1, "sample": 2}[kind]
                    cdeps = [bCONST] + ([bSC] if kind == "sample" else [])
                    mask = SMASK if kind == "sample" else DMASK
                    if MERGE01:
                        pk_, bpk_ = dpool.next()
                        ps_k = pk_[:, 0:256].bitcast(BF16)
                        ps_k3 = ps_k.rearrange("p (j i) -> p j i", j=4)
                    else:
                        bpk_ = bpt
                    for j in range(4):
                        tr(ps_k[0:L, j * 128:(j + 1) * 128], kt[:, j, t0:t0 + L], IDENT[:, :], [bkt, bCONST], [bpk_], j == 3)
                    for j in range(4):
                        act(KH[0:L, j, :], ps_k3[0:L, j, :], AF.Identity, [bCONST], [bpk_, bKH], scale=KDEC[0:L, kd, h0 + j:h0 + j + 1])
                    if not MERGE01:
                        yield
                    if part == "state":
                        yield
                    if part != "state":
                        for j in range(4):
                            mm(ps_[0:L, j * 128:j * 128 + L], kt[:, j, t0:t0 + L], qt[:, j, t0:t0 + L], True, True,
                               [bkt, bqt], [bps], j == 3)
                        tt("dve", PTm[0:L, :, 0:L], ps3[0:L, :, 0:L], mask[0:L, h0:h0 + 4, 0:L], ALU.mult, cdeps, [bps, bPTm])
                        if kind == "full":
                            tt(POOL_ENG, QTL[:, :, 0:L], qt[:, :, t0:t0 + L], QDEC[:, h0:h0 + 4, 0:L], ALU.mult, [bqt, bCONST], [bQTL])
                        elif kind == "sample":
                            tt("dve", QTL[:, :, 0:L], qt[:, :, t0:t0 + L], QDECS[:, h0:h0 + 4, 0:L], ALU.mult, [bqt] + cdeps, [bQTL])
                        yield
                        assert vdone[hh] > ci, "o-matmul stage emitted before V of this chunk"
                        if kind == "sample":
                            pstep = QM[:].ap[0][0]
                            def ld_bf(jj, hf):
                                dma("pool", SINBFH[hf][:], sret_in[l][:, h0 + jj, hf * 8:(hf + 1) * 8, :], ch_sinbf[hf], [], [bSINBFH[hf]])
                            ld_bf(0, 0)
                            ld_bf(0, 1)
                            for j in range(4):
                                h = h0 + j
                                qm_diag = bass.AP(QM, 0, [[pstep, 128], [136, NSS], [1, DS]])
                                q_src = QTL[:, j, :].rearrange("p (s i) -> p s i", s=NSS)
                                cp("dve", qm_diag, q_src, [bQTL], [bQM])
                                mm(ps_[:, j * 128:(j + 1) * 128], PTm[:, j, :], v[:, ci, j * 128:(j + 1) * 128], True, False,
                                   [bPTm, bv], [bps], False)
                                for s in range(NSS):
                                    mm(ps_[:, j * 128:(j + 1) * 128], QM[:, s, :], SINBFH[s // 8][:, s % 8, :], False, s == NSS - 1,
                                       [bQM, bSINBFH[s // 8]], [bps], s in (7, NSS - 1))
                                    if s in (7, NSS - 1) and j < 3:
                                        ld_bf(j + 1, s // 8)
                                yield
                        else:
                            first = kind == "lead"
                            for j in range(4):
                                h = h0 + j
                                mm(ps_[0:L, j * 128:(j + 1) * 128], PTm[0:L, j, 0:L], v[0:L, ci, j * 128:(j + 1) * 128], True, first,
                                   [bPTm, bv], [bps], first and j == 3)
                                if not first:
                                    mm(ps_[0:L, j * 128:(j + 1) * 128], QTL[:, j, 0:L], SBF[:, l, h * 128:(h + 1) * 128], False, True,
                                       [bQTL, bSBF[l][hh]], [bps], j == 3)
                        if kind != "sample":
                            psS, bpsS = dpool.next()
                            for j in range(4):
                                mm(psS[:, j * 128:(j + 1) * 128], KH[0:L, j, :], v[0:L, ci, j * 128:(j + 1) * 128], True, True,
                                   [bKH, bv], [bpsS], j == 3)
                            if kind == "lead":
                                cp("dve", SST[:, l, h0 * 128:(h0 + 4) * 128], psS[:, :], [], [bpsS, bSST[l][hh]])
                            else:
                                for j in range(4):
                                    gLj = float(np.exp(LOGG[h0 + j] * 128.0))
                                    sl = SST[:, l, (h0 + j) * 128:(h0 + j + 1) * 128]
                                    stt("dve", SBF[:, l, (h0 + j) * 128:(h0 + j + 1) * 128], sl, gLj, psS[:, j * 128:(j + 1) * 128],
                                        ALU.mult, ALU.add, [bSST[l][hh]], [bpsS, bSBF[l][hh]])
                                for j in range(4):
                                    gLj = float(np.exp(LOGG[h0 + j] * 128.0))
                                    sl = SST[:, l, (h0 + j) * 128:(h0 + j + 1) * 128]
                                    stt("dve", sl, sl, gLj, psS[:, j * 128:(j + 1) * 128], ALU.mult, ALU.add,
                                        [bSST[l][hh]], [bpsS, bSST[l][hh]])
                            if kind == "lead":
                                cp("act", SBF[:, l, h0 * 128:(h0 + 4) * 128], SST[:, l, h0 * 128:(h0 + 4) * 128], [bSST[l][hh]], [bSBF[l][hh]])
                            yield
                        act(SQO[0:L, :, :], ps3[0:L, :, :], AF.Square, [], [bps, bSQO])
                        P.op("dve", lambda e: e.tensor_reduce(out=STAT[0:L, 0, :], in_=ps3[0:L, :, :], axis=AX.X, op=ALU.add),
                             [], [bps, bSTAT])
                        P.op("dve", lambda e: e.tensor_reduce(out=STAT[0:L, 1, :], in_=SQO[0:L, :, :], axis=AX.X, op=ALU.add),
                             [bSQO], [bSTAT])
                        if not MERGE3:
                            yield
                        ts("dve", STAT[0:L, 2, :], STAT[0:L, 0, :], 1.0 / 128, None, ALU.mult, None, [bSTAT], [bSTAT])
                        tt("dve", STAT[0:L, 3, :], STAT[0:L, 2, :], STAT[0:L, 2, :], ALU.mult, [bSTAT], [bSTAT])
                        stt("dve", STAT[0:L, 4, :], STAT[0:L, 1, :], 1.0 / 128, STAT[0:L, 3, :], ALU.mult, ALU.subtract, [bSTAT], [bSTAT])
                        act(STAT[0:L, 5, :], STAT[0:L, 4, :], AF.Sqrt, [bSTAT, bCONST], [bSTAT], bias=EPSC[0:L, 0:1], scale=1.0)
                        P.op("dve", lambda e: e.reciprocal(out=STAT[0:L, 6, :], in_=STAT[0:L, 5, :]), [bSTAT], [bSTAT])
                        stt("dve", STAT[0:L, 7, :], STAT[0:L, 2, :], -1.0, STAT[0:L, 6, :], ALU.mult, ALU.mult, [bSTAT], [bSTAT])
                        yield
                        for j in range(4):
                            act(ON[0:L, j, :], ps3[0:L, j, :], AF.Identity, [bSTAT], [bps, bON],
                                bias=STAT[0:L, 7, j:j + 1], scale=STAT[0:L, 6, j:j + 1])
                        yield
                        for j in range(4):
                            tr(ps_t[:, j * 128:j * 128 + L], ON[0:L, j, :], IDENT[0:L, 0:L], [bON, bCONST], [bpt], j == 3)
                        yield
                        ob = [bO[h0 + j][g] for j in range(4)]
                        assert flags["g%d" % hh], "gate stage emitted before silu(g) of this half"
                        tt("dve", O[:, h0:h0 + 4, t0:t0 + L], ps_t3[:, :, 0:L], O[:, h0:h0 + 4, t0:t0 + L], ALU.mult, ob, [bpt] + ob)
                    if kind == "sample" and part != "o":
                        def ld_q(k):
                            dma("sp", SINQ[k % 4][:], sret_in[l][:, h0 + k // 4, (k % 4) * 4:(k % 4 + 1) * 4, :], ch_sin[k % 4], [], [bSINQ[k % 4]])
                        ld_q(0)
                        ld_q(1)
                        for k in range(16):
                            j, sq = k // 4, k % 4
                            h = h0 + j
                            if sq == 0:
                                kh_b = bass.AP(KH, KH[:, j, :].offset, [[KH[:].ap[0][0], 128], [0, NSS], [1, 128]])
                                tt("dve", KM[:], kh_b, RM3[:], ALU.mult, [bKH, bSC], [bKM])
                            if k + 2 < 16:
                                ld_q(k + 2)
                            g8 = float(np.exp(LOGG[h] * DS))
                            psq, bpsq = dpool.next()
                            for si in range(4):
                                s = sq * 4 + si
                                mm(psq[:, si * 128:(si + 1) * 128], KM[:, s, :], v[:, ci, j * 128:(j + 1) * 128], True, True,
                                   [bKM, bv], [bpsq], si == 3)
                            stt("dve", SINQ[sq][:], SINQ[sq][:], g8, psq[:].rearrange("p (s e) -> p s e", s=4), ALU.mult, ALU.add,
                                [bSINQ[sq]], [bpsq, bSINQ[sq]])
                            dma("sp", oss[l][:, h, sq * 4:(sq + 1) * 4, :], SINQ[sq][:], ch_sout[sq], [bSINQ[sq]], [])
                            if k % 2 == 1:
                                yield

                if DBG: print(f"[sim] sg{sgi} l{l} P1 start pe={P.t['pe']:.0f} dve={P.t['dve']:.0f}"); b0_ = dict(P.busy); t0_ = P.t['pe']
                vdone = [0, 0]
                flags = {"d1": False, "g0": False, "kq1": False, "g1": False}
                early = SG0_EARLY and has_sample
                run(dense_gen(0, ("k", "q")))
                if early:
                    run(dense_gen(1, ("k", "q")))
                    flags["kq1"] = True

                def side_chain():
                    if early:
                        for hh_ in range(2):
                            for _ in dense_gen(hh_, ("v",)):
                                yield
                        for hh_ in range(2):
                            for _ in dense_gen(hh_, ("g",)):
                                yield
                            flags["g%d" % hh_] = True
                        flags["d1"] = True
                        return
                    for _ in dense_gen(0, ("v",)):
                        yield
                    for _ in dense_gen(0, ("g",)):
                        yield
                    flags["g0"] = True
                    for _ in dense_gen(1, ("v", "q", "k", "g")):
                        yield
                    flags["kq1"] = True
                    flags["g1"] = True
                    flags["d1"] = True
                    if overlap_conv:
                        for _ in conv_gen(cx, dpool):
                            yield

                tasks_ = []
                order_ = [(hh_, ci) for ci in range(nch) for hh_ in range(2)] if early else \
                         [(hh_, ci) for hh_ in range(2) for ci in range(nch)]
                for hh_, ci in order_:
                    rdy = None if (hh_ == 0 or early) else (lambda: flags["kq1"])
                    if chunks[ci][0] == "sample" and SPLIT_SAMPLE:
                        tasks_.append(((lambda cs_, ci=ci, hh_=hh_: chunk_gen(hh_, ci, cs_, "o")), rdy))
                        tasks_.append(((lambda cs_, ci=ci, hh_=hh_: chunk_gen(hh_, ci, cs_, "state")), rdy))
                    else:
                        tasks_.append(((lambda cs_, ci=ci, hh_=hh_: chunk_gen(hh_, ci, cs_)), rdy))
                if early:
                    sr_fn = lambda: SIDE_RATIO + (2 if not flags["g1"] else 0)
                else:
                    sr_fn = lambda: SIDE_RATIO + (0 if flags["g0"] else 1)
                interleave(tasks_, csets, side=side_chain(), side_ratio=sr_fn)
                if sgi == 2:
                    dma("sp", osp[l], SST[:, l, :], ch_small, [bSST[l][0], bSST[l][1]], [])
                if DBG: print(f"[sim] sg{sgi} l{l} P1 done pe={P.t['pe']:.0f} dve={P.t['dve']:.0f} act={P.t['act']:.0f}", "dur", round(P.t['pe'] - t0_), "busy", {k_: round(P.busy[k_] - b0_.get(k_, 0)) for k_ in P.busy})
            if has_sample:
                P.barrier()

            if not overlap_conv:
                cx = conv_alloc()
                run(conv_gen(cx, AccPool([0, 1, 2, 3, 4, 5, 6, 7])))
            YB, bYB = cx["YB"], cx["bYB"]

            with ExitStack() as st2:
                R = sb("R", (128, NK, T), BF16, st2)
                bR = [Buf(f"R{g}") for g in range(2)]
                SG_ = [sb(f"SGa{i}", (128, 512), F32, st2) for i in range(2)]
                T1 = [sb(f"T1{i}", (128, 512), F32, st2) for i in range(2)]
                bSG = [Buf("SGa0"), Buf("SGa1")]
                bT1 = [Buf("T10"), Buf("T11")]
                pool = AccPool([0, 1, 2, 3, 4, 5])
                it = 0
                for mp in range(4):
                    w, bw = w_next("garo")
                    for mi in range(2):
                        m = 2 * mp + mi
                        for g, (a, n) in enumerate(ngs):
                            i = it % 2
                            it += 1
                            acc1, bacc1 = pool.next()
                            acc2, bacc2 = pool.next()
                            dense_acc(acc1, bacc1, w, bw, mi * 2048, NK, A, lambda kc, a, n: A[:, kc, a:a + n], a, n, AK(g))
                            dense_acc(acc2, bacc2, w, bw, mi * 2048 + 1024, NK, O, lambda kc, a, n: O[:, kc, a:a + n], a, n,
                                      [bO[h][g] for h in range(HEADS)])
                            act(SG_[i][:, 0:n], acc1[:, 0:n], AF.Sigmoid, [], [bacc1, bSG[i]])
                            tt("dve", R[:, m, a:a + n], acc2[:, 0:n], SG_[i][:, 0:n], ALU.mult, [bSG[i]], [bacc2, bR[g]])
                for mp in range(4):
                    w, bw = w_next("gbco")
                    for mi in range(2):
                        m = 2 * mp + mi
                        for g, (a, n) in enumerate(ngs):
                            i = it % 2
                            it += 1
                            acc1, bacc1 = pool.next()
                            acc2, bacc2 = pool.next()
                            dense_acc(acc1, bacc1, w, bw, mi * 2048, NK, A, lambda kc, a, n: A[:, kc, a:a + n], a, n, AK(g))
                            dense_acc(acc2, bacc2, w, bw, mi * 2048 + 1024, NK, YB, lambda kc, a, n: YB[:, kc, a:a + n], a, n, [bYB[g]])
                            act(SG_[i][:, 0:n], acc1[:, 0:n], AF.Sigmoid, [], [bacc1, bSG[i]])
                            tt("dve", T1[i][:, 0:n], acc2[:, 0:n], SG_[i][:, 0:n], ALU.mult, [bSG[i]], [bacc2, bT1[i]])
                            tt("dve", R[:, m, a:a + n], T1[i][:, 0:n], R[:, m, a:a + n], ALU.add, [bT1[i], bR[g]], [bR[g]])
                for wi in range(2):
                    w, bw = w_next("wo")
                    for mj in range(4):
                        m = wi * 4 + mj
                        for g, (a, n) in enumerate(ngs):
                            acc, bacc = pool.next()
                            dense_acc(acc, bacc, w, bw, mj * 1024, NK, R, lambda kc, a, n: R[:, kc, a:a + n], a, n, [bR[g]])
                            tt("dve", H[:, m, a:a + n], acc[:, 0:n], H[:, m, a:a + n], ALU.add, [bH[m][g]], [bacc, bH[m][g]])
                            rms_accum_lag(sg, g, m)
                rms_finish(sg, 2 + l, lambda kc, a, n: A[:, kc, a:a + n], lambda kc, g: bA[kc][g])
            conv_stack.close()

            if DBG: print(f"[sim] sg{sgi} l{l} P5/P6 done pe={P.t['pe']:.0f} dve={P.t['dve']:.0f}")
            with ExitStack() as st2:
                AB = sb("AB", (128, NF, T), BF16, st2)
                bAB = [Buf(f"AB{g}") for g in range(2)]
                SL = [sb(f"SL{i}", (128, 512), F32, st2) for i in range(2)]
                bSL = [Buf("SL0"), Buf("SL1")]
                pool = AccPool([0, 1, 2, 3, 4, 5])
                it = 0
                for fp in range(11):
                    w, bw = w_next("ffn1")
                    for g, (a, n) in enumerate(ngs):
                        for fi in range(2):
                            f = 2 * fp + fi
                            i = it % 2
                            it += 1
                            acc1, bacc1 = pool.next()
                            acc2, bacc2 = pool.next()
                            xa = lambda kc, a, n: A[:, kc, a:a + n]
                            dense_acc(acc1, bacc1, w, bw, fi * 2048, NK, A, xa, a, n, AK(g))
                            dense_acc(acc2, bacc2, w, bw, fi * 2048 + 1024, NK, A, xa, a, n, AK(g))
                            act(SL[i][:, 0:n], acc1[:, 0:n], AF.Silu, [], [bacc1, bSL[i]])
                            tt("dve", AB[:, f, a:a + n], acc2[:, 0:n], SL[i][:, 0:n], ALU.mult, [bSL[i]], [bacc2, bAB[g]])
                for m in range(NK):
                    w, bw = w_next("wd")
                    for g, (a, n) in enumerate(ngs):
                        acc, bacc = pool.next()
                        dense_acc(acc, bacc, w, bw, 0, NF, AB, lambda kc, a, n: AB[:, kc, a:a + n], a, n, [bAB[g]])
                        tt("dve", H[:, m, a:a + n], acc[:, 0:n], H[:, m, a:a + n], ALU.add, [bH[m][g]], [bacc, bH[m][g]])
                        rms_accum_lag(sg, g, m)

        sg_stack.close()
        if has_sample:
            P.barrier()
        with ExitStack() as st2:
            Y = sb("Y", (128, NK, T), F32, st2)
            bY = [[Buf(f"Y{k}_{g}") for g in range(2)] for k in range(NK)]
            rms_finish(sg, 4, lambda kc, a, n: Y[:, kc, a:a + n], lambda kc, g: bY[kc][g])
            for kc in range(NK):
                dma("sp", y_out[sgi][:, kc, :], Y[:, kc, 0:T], ch_y, [bY[kc][g] for g in range(len(ngs))], [])
            if sgi == 2:
                dma("sp", ocp[:], CARRY[:], ch_small, [b for l in range(DEPTH) for b in bCARRY[l]], [])
            if sgi == 0:
                dma("sp", ocs[:], CSS[:], ch_small, [b for l in range(DEPTH) for b in bCSS[l]], [])
            if sgi + 1 < len(SGS):
                load_sg(sgi + 1)
                rmsnorm(SGS[sgi + 1], 0, lambda kc, a, n: A[:, kc, a:a + n], lambda kc, g: bA[kc][g])
            P.barrier()

    assert wstate["cur"] == len(wseq) - 1
    print("[sim] engine clocks (us):", {k_: round(v_, 1) for k_, v_ in P.t.items()}, "busy", {k_: round(v_, 1) for k_, v_ in P.busy.items()})
    P.barrier(engines=("sp",))

    for s in list(P.src.values()) + P.chans:
        s.sem = es.enter_context(nc.semaphore(s.name))
    block = es.enter_context(nc.Block())
    P.emit(block)
    es.close()
    return nc


LOGG = np.log1p(-np.exp2(-5.0 - np.arange(HEADS, dtype=np.float64)))


def const_tables():
    f32 = np.float32
    scale = 128.0 ** -0.5
    i = np.arange(128)
    diff = i[None, :] - i[:, None]
    dmask = np.zeros((128, HEADS, 128), np.float64)
    smask = np.zeros((128, HEADS, 128), np.float64)
    same = (i[None, :] // DS) == (i[:, None] // DS)
    for h in range(HEADS):
        dec = np.where(diff >= 0, np.exp(LOGG[h] * np.maximum(diff, 0)), 0.0) * scale
        dmask[:, h, :] = dec
        smask[:, h, :] = np.where(same, dec, 0.0)
    qdec = np.zeros((128, HEADS, 128))
    qdecs = np.zeros((128, HEADS, 128))
    kdec = np.zeros((128, 3, HEADS))
    for h in range(HEADS):
        qdec[:, h, :] = np.exp(LOGG[h] * (i + 1.0))[None, :]
        qdecs[:, h, :] = np.exp(LOGG[h] * ((i % DS) + 1.0))[None, :]
        kdec[:, 0, h] = np.exp(LOGG[h] * (127.0 - i)) * scale
        kdec[:, 1, h] = np.exp(LOGG[h] * (15.0 - np.minimum(i, 15))) * scale
        kdec[:, 2, h] = np.exp(LOGG[h] * (DS - 1.0 - (i % DS))) * scale
    rm3 = np.zeros((128, NSS, 128))
    for s in range(NSS):
        rm3[s * DS:(s + 1) * DS, s, :] = 1.0
    inv = np.power(np.float32(10000.0), -np.arange(64, dtype=f32) / np.float32(64)).astype(f32)
    cs = []
    for sg in SGS:
        T = sg["T"]
        pos = np.zeros(T, f32)
        if sg["p0"] is None:
            pos[0:128] = PAST + (np.arange(128) % DS)
            pos[128:144] = np.arange(16)
            pos[144:] = 16 + np.arange(T - 144)
        else:
            pos[:] = N_META + sg["p0"] + np.arange(T)
        ang = (pos[:, None] * inv[None, :]).astype(f32)
        cos, sin = np.cos(ang).astype(f32).T, np.sin(ang).astype(f32).T
        tab = np.zeros((128, 2, T), f32)
        tab[0:64, 0], tab[64:128, 0] = cos, cos
        tab[0:64, 1], tab[64:128, 1] = sin, -sin
        cs.append(tab)
    return dict(dmask=dmask.astype(f32), smask=smask.astype(f32), qdec=qdec.astype(f32), qdecs=qdecs.astype(f32),
                kdec=kdec.astype(f32), rm3=rm3.astype(f32), ident=np.eye(128, dtype=f32)), cs


_CACHE = {}


def kernel(x_prompt, x_sample, state_ret, state_conv, meta_tokens, norm_mix_g, w_in, conv_w,
           w_ret_o, w_conv_o, w_o, norm_ffn_g, w_gate_up, w_down, final_norm_g):
    f32 = np.float32
    A_ = lambda v: np.asarray(v, dtype=f32)
    x_prompt, x_sample, state_ret, state_conv, meta_tokens = map(A_, (x_prompt, x_sample, state_ret, state_conv, meta_tokens))
    w_in, w_ret_o, w_conv_o, w_o, w_gate_up, w_down = map(A_, (w_in, w_ret_o, w_conv_o, w_o, w_gate_up, w_down))
    norm_mix_g, norm_ffn_g, final_norm_g, conv_w = map(A_, (norm_mix_g, norm_ffn_g, final_norm_g, conv_w))

    if "nc" not in _CACHE:
        _CACHE["nc"] = build_program()
    nc = _CACHE["nc"]

    consts, cs = const_tables()
    wpk = pack_weights(w_in, w_ret_o, w_conv_o, w_o, w_gate_up, w_down)
    gains = np.stack([norm_mix_g[0], norm_mix_g[1], norm_ffn_g[0], norm_ffn_g[1], final_norm_g], 0)
    gains = np.ascontiguousarray(gains.reshape(5, NK, 128).transpose(2, 0, 1))
    convw = np.ascontiguousarray(conv_w.reshape(DEPTH, 3, NK, 128).transpose(3, 0, 2, 1))

    def fm(X):
        return np.ascontiguousarray(X.T.reshape(NK, 128, X.shape[0]).transpose(1, 0, 2))

    in_maps = []
    for c in range(NCORES):
        xs = x_sample[c * NSS:(c + 1) * NSS].reshape(NSS * DS, D)
        X0 = np.concatenate([xs, meta_tokens, x_prompt[c, 0:512]], 0)
        X1 = x_prompt[c, 512:1280]
        X2 = x_prompt[c, 1280:2048]
        sret = np.ascontiguousarray(state_ret[:, c * NSS:(c + 1) * NSS].transpose(0, 3, 2, 1, 4))
        sconv = np.ascontiguousarray(state_conv[:, c * NSS:(c + 1) * NSS].reshape(DEPTH, NSS, 2, NK, 128).transpose(4, 0, 3, 1, 2))
        m = dict(x0=fm(X0), x1=fm(X1), x2=fm(X2), cs0=cs[0], cs1=cs[1], cs2=cs[2], wpk=wpk,
                 gains=gains, convw=convw, sret=sret, sconv=sconv)
        m.update(consts)
        in_maps.append(m)

    res = run_bass_kernel_spmd(nc, in_maps, core_ids=list(range(NCORES)))
    R = res.results

    y_prompt = np.empty((NCORES, SEQ, D), f32)
    y_sample = np.empty((NCORES * NSS, DS, D), f32)
    s_p = np.empty((DEPTH, NCORES, HEADS, 128, 128), f32)
    c_p = np.empty((DEPTH, NCORES, 2, D), f32)
    s_s = np.empty((DEPTH, NCORES * NSS, HEADS, 128, 128), f32)
    c_s = np.empty((DEPTH, NCORES * NSS, 2, D), f32)

    def tm(Y):
        return Y.transpose(2, 1, 0).reshape(Y.shape[2], D)

    for c in range(NCORES):
        r = R[c]
        Y0, Y1, Y2 = tm(r["y0"]), tm(r["y1"]), tm(r["y2"])
        y_sample[c * NSS:(c + 1) * NSS] = Y0[0:128].reshape(NSS, DS, D)
        y_prompt[c, 0:512] = Y0[144:656]
        y_prompt[c, 512:1280] = Y1
        y_prompt[c, 1280:2048] = Y2
        s_p[:, c] = r["osp"].reshape(DEPTH, 128, HEADS, 128).transpose(0, 2, 1, 3)
        c_p[:, c] = r["ocp"].transpose(1, 3, 2, 0).reshape(DEPTH, 2, D)
        s_s[:, c * NSS:(c + 1) * NSS] = r["oss"].transpose(0, 3, 2, 1, 4)
        c_s[:, c * NSS:(c + 1) * NSS] = r["ocs"].transpose(1, 3, 4, 2, 0).reshape(DEPTH, NSS, 2, D)
    return (y_prompt, y_sample, s_p, c_p, s_s, c_s)
```
